# Optimizing a Trainium2 kernel written in Bass

```python
import jax
import jax.numpy as jnp
from jax import lax
import numpy as np

D_MODEL = 1024
BATCH = 16
SEQ = 256
DEPTH = 4
DEC_BATCH = 4
DEC_SEQ = 2048
PAST_LEN = 256

GRID_W = 64
N_MIXERS = 3
N_POOL = len(range(0, DEPTH, N_MIXERS))
N_MLA = len(range(1, DEPTH, N_MIXERS))
N_NA = len(range(2, DEPTH, N_MIXERS))

POOL_WINDOWS = (2, 4, 8, 16)
POOL_GROUPS = len(POOL_WINDOWS)
POOL_GROUP_DIM = D_MODEL // POOL_GROUPS

MLA_HEADS = 8
QK_NOPE = 128
QK_ROPE = 64
V_HEAD = 128
Q_LORA = D_MODEL // 2
KV_LORA = D_MODEL // 4
MLA_WIDTH = MLA_HEADS * V_HEAD
MLA_SCALE = (QK_NOPE + QK_ROPE) ** -0.5

NA_HEADS = 16
NA_HEAD_DIM = 64
NA_WIDTH = NA_HEADS * NA_HEAD_DIM
NA_WIN_R = 8
NA_WIN_C = 16
NA_Q_COLS = 16
NA_K_COLS = 2 * NA_WIN_C
NA_SCALE = NA_HEAD_DIM ** -0.5

Q_BLOCK = 128
ROPE_BASE = 10000.0
LN_EPS = 1e-5
RMS_EPS = 1e-6
NEG_INF = -1e30
DEEPNORM_ALPHA = (2 * DEPTH) ** 0.25
DEEPNORM_BETA = (8 * DEPTH) ** -0.25

kernel_name = 'hybrid_diffusion_pool_mla_natten_step'


def layer_norm(x, g, b):
    xf = x.astype(jnp.float32)
    mu = jnp.mean(xf, -1, keepdims=True)
    var = jnp.mean(jnp.square(xf - mu), -1, keepdims=True)
    return ((xf - mu) * lax.rsqrt(var + LN_EPS) * g + b).astype(x.dtype)


def rms_norm(x, g):
    xf = x.astype(jnp.float32)
    return (xf * lax.rsqrt(jnp.mean(xf * xf, -1, keepdims=True) + RMS_EPS) * g).astype(x.dtype)


def adaln(cond, w, b):
    m = jax.nn.silu(cond) @ w + b
    return jnp.split(m[:, None, :], 3, axis=-1)


def axial_rope(x):
    L, dr = x.shape[1], x.shape[-1]
    nf = dr // 4
    inv = ROPE_BASE ** (-jnp.arange(nf, dtype=jnp.float32) / nf)
    t = jnp.arange(L)
    pos = jnp.stack([t // GRID_W, t % GRID_W], -1).astype(jnp.float32)
    ang = pos[:, :, None] * inv
    shape = (1, L) + (1,) * (x.ndim - 3) + (2, nf)
    cos = jnp.cos(ang).reshape(shape)
    sin = jnp.sin(ang).reshape(shape)
    xs = x.astype(jnp.float32).reshape(x.shape[:-1] + (2, 2 * nf))
    x1, x2 = jnp.split(xs, 2, axis=-1)
    out = jnp.concatenate([x1 * cos - x2 * sin, x1 * sin + x2 * cos], -1)
    return out.reshape(x.shape).astype(x.dtype)


def blocked_attention(q, k, v, scale):
    B, Lq, H, dk = q.shape
    nb = Lq // Q_BLOCK
    qb = q.reshape(B, nb, Q_BLOCK, H, dk).transpose(1, 0, 2, 3, 4)

    def one_block(qblk):
        s = jnp.einsum('bqhd,bkhd->bhqk', qblk, k).astype(jnp.float32) * scale
        p = jax.nn.softmax(s, axis=-1)
        return jnp.einsum('bhqk,bkhd->bqhd', p.astype(v.dtype), v)

    o = lax.map(one_block, qb)
    return o.transpose(1, 0, 2, 3, 4).reshape(B, Lq, H, v.shape[-1])


def pool_branch(h, w_in, w_grp, scale, w_out):
    u, z = jnp.split(h @ w_in, 2, axis=-1)
    B, L, _ = u.shape
    ug = u.reshape(B, L, POOL_GROUPS, POOL_GROUP_DIM).astype(jnp.float32)
    csum = jnp.concatenate([jnp.zeros_like(ug[:, :1]), jnp.cumsum(ug, axis=1)], axis=1)
    t = jnp.arange(L)[:, None]
    w = jnp.array(POOL_WINDOWS)[None, :]
    lo = jnp.clip(t - w // 2, 0, L)
    hi = jnp.clip(t + w - w // 2, 0, L)
    gi = jnp.arange(POOL_GROUPS)[None, :]
    win_sum = csum[:, hi, gi] - csum[:, lo, gi]
    cnt = (hi - lo).astype(jnp.float32)[None, :, :, None]
    mixed = (win_sum / cnt - ug).astype(h.dtype)
    mixed = jnp.einsum('blgc,gcd->blgd', mixed, w_grp).reshape(B, L, D_MODEL) * scale
    return (mixed * jax.nn.silu(z)) @ w_out


def mla_inputs(h, w_in, q_norm, w_uq, kv_norm):
    B, L, _ = h.shape
    cq, ckv, kr, z = jnp.split(h @ w_in, [Q_LORA, Q_LORA + KV_LORA, Q_LORA + KV_LORA + QK_ROPE], axis=-1)
    q = (rms_norm(cq, q_norm) @ w_uq).reshape(B, L, MLA_HEADS, QK_NOPE + QK_ROPE)
    return q, rms_norm(ckv, kv_norm), kr, z


def mla_keys_values(ckv, kr, w_ukv):
    B, L, _ = ckv.shape
    kv = (ckv @ w_ukv).reshape(B, L, MLA_HEADS, QK_NOPE + V_HEAD)
    k_nope, v = jnp.split(kv, [QK_NOPE], axis=-1)
    k = jnp.concatenate([k_nope, jnp.broadcast_to(kr[:, :, None, :], (B, L, MLA_HEADS, QK_ROPE))], axis=-1)
    return k, v


def mla_context(h, w_in, q_norm, w_uq, kv_norm, w_ukv, w_out):
    B, L, _ = h.shape
    q, ckv, kr, z = mla_inputs(h, w_in, q_norm, w_uq, kv_norm)
    k, v = mla_keys_values(ckv, kr, w_ukv)
    o = blocked_attention(q, k, v, MLA_SCALE)
    y = (o.reshape(B, L, MLA_WIDTH) * jax.nn.silu(z)) @ w_out
    return y, ckv, kr


def mla_latent(h, ckv_ctx, kr_ctx, w_in, q_norm, w_uq, kv_norm, w_ukv, w_out):
    B, L, _ = h.shape
    q, ckv, kr, z = mla_inputs(h, w_in, q_norm, w_uq, kv_norm)
    q = jnp.concatenate([q[..., :QK_NOPE], axial_rope(q[..., QK_NOPE:])], axis=-1)
    k_ctx, v_ctx = mla_keys_values(ckv_ctx, kr_ctx, w_ukv)
    k_lat, v_lat = mla_keys_values(ckv, axial_rope(kr), w_ukv)
    k = jnp.concatenate([k_ctx, k_lat], axis=1)
    v = jnp.concatenate([v_ctx, v_lat], axis=1)
    o = blocked_attention(q, k, v, MLA_SCALE)
    return (o.reshape(B, L, MLA_WIDTH) * jax.nn.silu(z)) @ w_out


def na_inputs(h, w_in):
    B, L, _ = h.shape
    q, k, v, z = jnp.split(h @ w_in, 4, axis=-1)
    shp = (B, L, NA_HEADS, NA_HEAD_DIM)
    return q.reshape(shp), k.reshape(shp), v.reshape(shp), z


def na_context(h, w_in, w_out):
    B, L, _ = h.shape
    q, k, v, z = na_inputs(h, w_in)
    o = blocked_attention(q, k, v, NA_SCALE)
    y = (o.reshape(B, L, NA_WIDTH) * jax.nn.silu(z)) @ w_out
    return y, k, v


def neighbourhood_attention(q, k, v, k_ctx, v_ctx, rpb):
    B, L, H, dh = q.shape
    rows = L // GRID_W
    kr = min(NA_WIN_R, rows)
    ncb = GRID_W // NA_Q_COLS
    qcols = np.arange(GRID_W).reshape(ncb, NA_Q_COLS)
    cstart = np.clip(qcols - NA_WIN_C // 2, 0, GRID_W - NA_WIN_C)
    bstart = np.clip(np.arange(ncb) * NA_Q_COLS - NA_WIN_C // 2, 0, GRID_W - NA_K_COLS)
    kcols = bstart[:, None] + np.arange(NA_K_COLS)
    col_valid = (kcols[:, None, :] >= cstart[:, :, None]) & (kcols[:, None, :] < cstart[:, :, None] + NA_WIN_C)
    dc_idx = np.clip(kcols[:, None, :] - qcols[:, :, None] + NA_WIN_C - 1, 0, 2 * NA_WIN_C - 2)
    qg = q.reshape(B, rows, GRID_W, H, dh).transpose(1, 0, 2, 3, 4)
    kg = k.reshape(B, rows, GRID_W, H, dh)
    vg = v.reshape(B, rows, GRID_W, H, dh)
    n_loc = kr * NA_K_COLS

    def row_block(args):
        r, q_r = args
        rs = jnp.clip(r - kr // 2, 0, rows - kr)
        kb = lax.dynamic_slice_in_dim(kg, rs, kr, axis=1)[:, :, kcols]
        vb = lax.dynamic_slice_in_dim(vg, rs, kr, axis=1)[:, :, kcols]
        qb = q_r.reshape(B, ncb, NA_Q_COLS, H, dh)
        s_loc = jnp.einsum('bjqhd,bajkhd->bhjqak', qb, kb).astype(jnp.float32) * NA_SCALE
        dr_idx = rs + jnp.arange(kr) - r + NA_WIN_R - 1
        bias = rpb[:, dr_idx[:, None, None, None], dc_idx[None]]
        s_loc = s_loc + bias.transpose(0, 2, 3, 1, 4)[None].astype(jnp.float32)
        s_loc = jnp.where(col_valid[None, None, :, :, None, :], s_loc, NEG_INF)
        s_loc = s_loc.reshape(B, H, ncb, NA_Q_COLS, n_loc)
        s_ctx = jnp.einsum('bjqhd,bchd->bhjqc', qb, k_ctx).astype(jnp.float32) * NA_SCALE
        p = jax.nn.softmax(jnp.concatenate([s_loc, s_ctx], axis=-1), axis=-1).astype(v.dtype)
        p_loc = p[..., :n_loc].reshape(B, H, ncb, NA_Q_COLS, kr, NA_K_COLS)
        o = jnp.einsum('bhjqak,bajkhd->bjqhd', p_loc, vb) + jnp.einsum('bhjqc,bchd->bjqhd', p[..., n_loc:], v_ctx)
        return o.reshape(B, GRID_W, H, dh)

    o = lax.map(row_block, (jnp.arange(rows), qg))
    return o.transpose(1, 0, 2, 3, 4).reshape(B, L, H * dh)


def na_latent(h, k_ctx, v_ctx, w_in, rpb, w_out):
    q, k, v, z = na_inputs(h, w_in)
    o = neighbourhood_attention(q, k, v, k_ctx, v_ctx, rpb)
    return (o * jax.nn.silu(z)) @ w_out


def setup_inputs(seed: int = 0) -> dict:
    key = jax.random.key(seed)
    ks = jax.random.split(key, 25)
    f32 = jnp.float32
    nrm = lambda k, shape, s=1.0: jax.random.normal(k, shape, f32) * s
    D = D_MODEL
    return {
        'x_prompt': nrm(ks[0], (BATCH, SEQ, D)),
        'x_sample': nrm(ks[1], (DEC_BATCH, DEC_SEQ, D)),
        'cache_mla_ckv': nrm(ks[2], (DEC_BATCH, N_MLA, PAST_LEN, KV_LORA)),
        'cache_mla_krope': nrm(ks[3], (DEC_BATCH, N_MLA, PAST_LEN, QK_ROPE)),
        'cache_na_k': nrm(ks[4], (DEC_BATCH, N_NA, PAST_LEN, NA_HEADS, NA_HEAD_DIM)),
        'cache_na_v': nrm(ks[5], (DEC_BATCH, N_NA, PAST_LEN, NA_HEADS, NA_HEAD_DIM)),
        'c': nrm(ks[6], (DEC_BATCH, D)),
        'c_ctx': nrm(ks[7], (D,)),
        'ada_w': nrm(ks[8], (DEPTH, D, 3 * D), 0.5 * D ** -0.5),
        'ada_b': nrm(ks[9], (DEPTH, 3 * D), 0.02),
        'ln_g': 1.0 + nrm(ks[10], (DEPTH, D), 0.02),
        'ln_b': nrm(ks[11], (DEPTH, D), 0.02),
        'pool_w_in': nrm(ks[12], (N_POOL, D, 2 * D), D ** -0.5),
        'pool_w_grp': nrm(ks[13], (N_POOL, POOL_GROUPS, POOL_GROUP_DIM, POOL_GROUP_DIM), POOL_GROUP_DIM ** -0.5),
        'pool_scale': 1.0 + nrm(ks[14], (N_POOL, D), 0.02),
        'pool_w_out': nrm(ks[15], (N_POOL, D, D), DEEPNORM_BETA * D ** -0.5),
        'mla_w_in': nrm(ks[16], (N_MLA, D, Q_LORA + KV_LORA + QK_ROPE + MLA_WIDTH), D ** -0.5),
        'mla_q_norm': 1.0 + nrm(ks[17], (N_MLA, Q_LORA), 0.02),
        'mla_w_uq': nrm(ks[18], (N_MLA, Q_LORA, MLA_HEADS * (QK_NOPE + QK_ROPE)), Q_LORA ** -0.5),
        'mla_kv_norm': 1.0 + nrm(ks[19], (N_MLA, KV_LORA), 0.02),
        'mla_w_ukv': nrm(ks[20], (N_MLA, KV_LORA, MLA_HEADS * (QK_NOPE + V_HEAD)), KV_LORA ** -0.5),
        'mla_w_out': nrm(ks[21], (N_MLA, MLA_WIDTH, D), DEEPNORM_BETA * MLA_WIDTH ** -0.5),
        'na_w_in': nrm(ks[22], (N_NA, D, 4 * NA_WIDTH), D ** -0.5),
        'na_rpb': nrm(ks[23], (N_NA, NA_HEADS, 2 * NA_WIN_R - 1, 2 * NA_WIN_C - 1), 0.1),
        'na_w_out': nrm(ks[24], (N_NA, NA_WIDTH, D), DEEPNORM_BETA * NA_WIDTH ** -0.5),
    }


def reference(x_prompt, x_sample, cache_mla_ckv, cache_mla_krope, cache_na_k, cache_na_v, c, c_ctx,
              ada_w, ada_b, ln_g, ln_b, pool_w_in, pool_w_grp, pool_scale, pool_w_out,
              mla_w_in, mla_q_norm, mla_w_uq, mla_kv_norm, mla_w_ukv, mla_w_out,
              na_w_in, na_rpb, na_w_out):
    yp = x_prompt
    ys = x_sample
    st_ckv, st_kr, st_k, st_v = [], [], [], []
    for i in range(DEPTH):
        kind, j = i % N_MIXERS, i // N_MIXERS
        sh_p, sc_p, g_p = adaln(c_ctx[None], ada_w[i], ada_b[i])
        sh_s, sc_s, g_s = adaln(c, ada_w[i], ada_b[i])
        hp = yp * (1.0 + sc_p) + sh_p
        hs = ys * (1.0 + sc_s) + sh_s
        if kind == 0:
            op = pool_branch(hp, pool_w_in[j], pool_w_grp[j], pool_scale[j], pool_w_out[j])
            os_ = pool_branch(hs, pool_w_in[j], pool_w_grp[j], pool_scale[j], pool_w_out[j])
        elif kind == 1:
            op, ckv, kr = mla_context(hp, mla_w_in[j], mla_q_norm[j], mla_w_uq[j], mla_kv_norm[j], mla_w_ukv[j], mla_w_out[j])
            st_ckv.append(ckv)
            st_kr.append(kr)
            os_ = mla_latent(hs, cache_mla_ckv[:, j], cache_mla_krope[:, j], mla_w_in[j], mla_q_norm[j],
                             mla_w_uq[j], mla_kv_norm[j], mla_w_ukv[j], mla_w_out[j])
        else:
            op, kc, vc = na_context(hp, na_w_in[j], na_w_out[j])
            st_k.append(kc)
            st_v.append(vc)
            os_ = na_latent(hs, cache_na_k[:, j], cache_na_v[:, j], na_w_in[j], na_rpb[j], na_w_out[j])
        yp = layer_norm(DEEPNORM_ALPHA * yp + g_p * op, ln_g[i], ln_b[i])
        ys = layer_norm(DEEPNORM_ALPHA * ys + g_s * os_, ln_g[i], ln_b[i])
    state_mla_ckv = jnp.stack(st_ckv, axis=1)
    state_mla_krope = jnp.stack(st_kr, axis=1)
    state_na_k = jnp.stack(st_k, axis=1)
    state_na_v = jnp.stack(st_v, axis=1)
    return (yp, ys, state_mla_ckv, state_mla_krope, state_na_k, state_na_v)
```

```python
import numpy as np
import concourse.bass as bass
import concourse.mybir as mybir
from concourse.bass_utils import run_bass_kernel_spmd
from contextlib import ExitStack

F32 = mybir.dt.float32
BF16 = mybir.dt.bfloat16
AF = mybir.ActivationFunctionType
ALU = mybir.AluOpType

D = 1024
DEPTH = 4
GRID_W = 64
ALPHA = (2 * DEPTH) ** 0.25
LN_EPS = 1e-5
RMS_EPS = 1e-6
MLA_SCALE = (128 + 64) ** -0.5
NA_SCALE = 64 ** -0.5
POOL_WINDOWS = (2, 4, 8, 16)
NEG = -1e30
NTAB = 14


class Buf:
    __slots__ = ("name", "w", "rs", "excl")

    def __init__(self, name="", excl=False):
        self.name = name
        self.w = None
        self.rs = []
        self.excl = excl


class Ins:
    __slots__ = ("q", "fn", "deps", "inc", "sem", "val", "isdma")


class Prog:
    QS = ("pe", "act", "dve", "pool", "sp")
    NDMA = 8

    def __init__(self, nc, es):
        self.nc = nc
        self.lists = {q: [] for q in self.QS}
        self.sems = {q: es.enter_context(nc.semaphore("s_" + q)) for q in ("pe", "act", "dve", "pool")}
        self.dsems = {q: [es.enter_context(nc.semaphore(f"d_{q}{i}")) for i in range(self.NDMA)]
                      for q in ("sp", "pool")}
        self.dcount = {q: 0 for q in self.dsems}
        self.dlast = {q: [None] * self.NDMA for q in self.dsems}
        self.out_dmas = []
        self.fence = []

    def barrier(self):
        self.marks = getattr(self, "marks", [])
        self.marks.append(len(self.lists["pe"]))
        f = []
        for q in self.QS:
            for ins in reversed(self.lists[q]):
                if not ins.isdma:
                    f.append(ins)
                    break
        for q in self.dlast:
            f.extend(i for i in self.dlast[q] if i is not None)
        self.fence = f

    def _add(self, q, fn, reads, writes, isdma=False):
        ins = Ins()
        ins.q = q; ins.fn = fn; ins.deps = []; ins.inc = False
        ins.isdma = isdma; ins.val = None
        writes = writes + [b for b in reads if b.excl and b not in writes]
        reads = [b for b in reads if not b.excl]
        deps = list(self.fence)
        for b in reads:
            if b.w is not None:
                deps.append(b.w)
        for b in writes:
            if b.w is not None:
                deps.append(b.w)
            deps.extend(b.rs)
        if isdma:
            k = self.dcount[q] % self.NDMA
            self.dcount[q] += 1
            prev = self.dlast[q][k]
            if prev is not None:
                deps.append(prev)
            self.dlast[q][k] = ins
            ins.sem = ("d", q, k)
            ins.inc = True
        else:
            ins.sem = ("c", q)
        seen = set()
        for d in deps:
            if d is ins or id(d) in seen:
                continue
            seen.add(id(d))
            if (not d.isdma) and (not isdma) and d.q == "pe" and q == "pe":
                continue
            d.inc = True
            ins.deps.append(d)
        for b in reads:
            if not isdma:
                b.rs = [r for r in b.rs if r.isdma or r.q != q]
            b.rs.append(ins)
        for b in writes:
            b.w = ins
            b.rs = []
        self.lists[q].append(ins)
        return ins

    def op(self, q, fn, reads=(), writes=()):
        return self._add(q, fn, list(reads), list(writes))

    def dma(self, q, out, in_, reads=(), writes=(), is_out=False):
        ins = self._add(q, lambda e: e.dma_start(out=out, in_=in_), list(reads), list(writes), isdma=True)
        if is_out:
            self.out_dmas.append(ins)
        return ins

    def emit(self):
        nc = self.nc
        for q in self.QS:
            c = 0
            dc = {}
            for ins in self.lists[q]:
                if ins.isdma:
                    dc[ins.sem] = dc.get(ins.sem, 0) + 16
                    ins.val = dc[ins.sem]
                elif ins.inc:
                    c += 1
                    ins.val = c

        def semobj(key):
            return self.sems[key[1]] if key[0] == "c" else self.dsems[key[1]][key[2]]

        def run(q, eng):
            waited = {}
            for ins in self.lists[q]:
                need = {}
                for d in ins.deps:
                    if d.val > need.get(d.sem, 0):
                        need[d.sem] = d.val
                for k, v in need.items():
                    if waited.get(k, 0) >= v:
                        continue
                    eng.wait_ge(semobj(k), v)
                    waited[k] = v
                r = ins.fn(eng)
                if ins.inc:
                    r.then_inc(semobj(ins.sem), 16 if ins.isdma else 1)
            if q == "sp":
                fin = {}
                for o in self.out_dmas:
                    if o.val > fin.get(o.sem, 0):
                        fin[o.sem] = o.val
                for k, v in fin.items():
                    eng.wait_ge(semobj(k), v)

        with nc.allow_non_contiguous_dma(reason="tiny transposed parameter vectors"), nc.Block() as block:
            @block.tensor
            def _(e):
                run("pe", e)

            @block.scalar
            def _(e):
                run("act", e)

            @block.vector
            def _(e):
                run("dve", e)

            @block.gpsimd
            def _(e):
                run("pool", e)

            @block.sync
            def _(e):
                run("sp", e)


class Grp:
    def __init__(self, gi, NT, seqs, sample, NKL=None, win=False):
        self.gi = gi
        self.NT = NT
        self.nblk = NT // 512
        self.seqs = seqs
        self.sample = sample
        self.win = win
        self.NKL = NT if NKL is None else NKL
        self.nkblk = self.NKL // 512
        self.NK = self.NKL + (256 if sample else 0)


def build_program(stop=99, dbg=False):
    nc = bass.Bass("TRN2", target_bir_lowering=False)

    def din(name, shape):
        return nc.dram_tensor(name, list(shape), F32, kind="ExternalInput").ap()

    def dout(name, shape):
        return nc.dram_tensor(name, list(shape), F32, kind="ExternalOutput").ap()

    I = {}
    for name, shape in [
        ("xp", (512, D)), ("xs", (2048, D)), ("cond", (2, D)),
        ("c_ckv", (256, 256)), ("c_kr", (256, 64)), ("c_nak", (256, D)), ("c_nav", (256, D)),
        ("ada_w", (4, D, 3 * D)), ("ada_b", (4, 3 * D)), ("ln_g", (4, D)), ("ln_b", (4, D)),
        ("pool_w_in", (2, D, 2 * D)), ("pool_w_grp", (2, 4, 256, 256)), ("pool_scale", (2, D)),
        ("pool_w_out", (2, D, D)),
        ("mla_w_in", (D, 1856)), ("mla_w_krp", (D, 64)), ("mla_q_norm", (512,)),
        ("mla_wq_pack", (8, 512, 256)), ("mla_kv_norm", (256,)), ("mla_w_ukv", (256, 2048)),
        ("mla_w_out", (D, D)),
        ("na_w_pack", (8, D, 512)), ("na_w_out", (D, D)), ("na_tab", (2, 16, 128, NTAB * 64)),
        ("rope_cos", (64, 2048)), ("rope_sin", (64, 2048)),
        ("rcnt_p", (4, 512)), ("rcnt_s", (4, 2048)), ("rcnt_w", (4, 1536)), ("ident", (128, 128)),
        ("rope_cos_w", (64, 1536)), ("rope_sin_w", (64, 1536)), ("wsel", (128, 2)),
    ]:
        I[name] = din(name, shape)
    O = {
        "yp": dout("yp", (512, D)), "ys": dout("ys", (1536, D)),
        "st_ckv": dout("st_ckv", (512, 256)), "st_kr": dout("st_kr", (512, 64)),
        "st_k": dout("st_k", (512, D)), "st_v": dout("st_v", (512, D)),
    }

    with ExitStack() as es:
        P = Prog(nc, es)

        def T(name, shape, dt):
            return es.enter_context(nc.sbuf_tensor("sb_" + name, list(shape), dt))

        ident = T("ident", [128, 128], F32)
        ones_bf = T("ones_bf", [128, 128], BF16)
        epsc = T("epsc", [128, 2], F32)
        ident_bf = T("ident_bf", [128, 128], BF16)
        lngT = T("lngT", [128, 4, 8], F32)
        lnbT = T("lnbT", [128, 4, 8], F32)
        adabT = T("adabT", [128, 4, 24], F32)
        pscT = T("pscT", [128, 2, 8], F32)
        qnT = T("qnT", [128, 4], F32)
        kvnT = T("kvnT", [128, 2], F32)
        condT = T("condT", [128, 2, 8], F32)
        wsel = T("wsel", [128, 2], F32)
        scondT = T("scondT", [128, 8, 2], BF16)
        mT = T("mT", [128, 4, 24, 2], F32)
        modv = T("modv", [128, 4, 2, 5, 8], F32)
        Yp = T("Yp", [128, 8, 2048], F32)
        H = T("H", [128, 8, 2048], BF16)
        ARENA_W = 27400
        arena = T("arena", [128, ARENA_W], F32)
        psb = [es.enter_context(nc.psum_tensor(f"ps{i}", [128, 512], F32)) for i in range(8)]
        psB = [Buf(f"ps{i}", excl=True) for i in range(8)]

        b_const = Buf("const")
        b_modv = Buf("modv")
        YpB = [[Buf() for _ in range(4)] for _ in range(8)]
        HB = [[Buf() for _ in range(4)] for _ in range(8)]

        class Arena:
            def __init__(self):
                self.off = 0

            def f32(self, n, shape=None):
                o = self.off
                self.off += n
                assert self.off <= ARENA_W, ("arena overflow", self.off)
                return arena[:, o:o + n]

            def bf16(self, n):
                w = (n + 1) // 2
                o = self.off
                self.off += w
                assert self.off <= ARENA_W, ("arena overflow", self.off)
                return arena[:, o:o + w].bitcast(BF16)[:, 0:n]

        def mm(out, lhsT, rhs, start, stop, reads, writes):
            P.op("pe", lambda e: e.matmul(out, lhsT, rhs, start=start, stop=stop), reads, writes)

        def tr(out, in_, idn, reads, writes):
            P.op("pe", lambda e: e.transpose(out, in_, idn), reads, writes)

        def act(out, in_, func, reads, writes, scale=None, bias=None):
            kw = {}
            if scale is not None:
                kw["scale"] = scale
            if bias is not None:
                kw["bias"] = bias
            P.op("act", lambda e: e.activation(out=out, in_=in_, func=func, **kw), reads, writes)

        def tt(out, in0, in1, op, reads, writes, q="dve"):
            P.op(q, lambda e: e.tensor_tensor(out=out, in0=in0, in1=in1, op=op), reads, writes)

        def ts(out, in0, s1, op0, reads, writes, s2=None, op1=None, q="dve"):
            if op1 is None:
                P.op(q, lambda e: e.tensor_scalar(out=out, in0=in0, scalar1=s1, scalar2=None, op0=op0), reads, writes)
            else:
                P.op(q, lambda e: e.tensor_scalar(out=out, in0=in0, scalar1=s1, scalar2=s2, op0=op0, op1=op1),
                     reads, writes)

        def stt(out, in0, scalar, in1, op0, op1, reads, writes):
            P.op("dve", lambda e: e.scalar_tensor_tensor(out=out, in0=in0, scalar=scalar, in1=in1, op0=op0, op1=op1),
                 reads, writes)

        def recip(out, in_, reads, writes):
            P.op("dve", lambda e: e.reciprocal(out=out, in_=in_), reads, writes)

        def recip_fast(out, in_, reads, writes):
            P.op("dve", lambda e: e.reciprocal_approx_fast(out=out, in_=in_), reads, writes)

        def memset(ap, v, writes, q="dve"):
            P.op(q, lambda e: e.memset(ap, v), (), writes)

        def load_cols(dst, src, n, width=128):
            bufs = []
            for i in range(n):
                b = Buf()
                P.dma("pool", dst[:, :, i * width:(i + 1) * width],
                      src[:, i * width:(i + 1) * width].rearrange("(c p) f -> p c f", p=128), writes=[b])
                bufs.append(b)
            return bufs

        mmrr = [0]
        mmpool = [[0, 1, 2, 3, 6, 7]]

        def mmbank():
            pool_ = mmpool[0]
            i = pool_[mmrr[0] % len(pool_)]
            mmrr[0] += 1
            return i

        P.dma("sp", ident[:], I["ident"], writes=[b_const])
        memset(ones_bf[:], 1.0, [b_const])
        act(ident_bf[:], ident[:], AF.Identity, [b_const], [b_const])
        memset(epsc[:, 0:1], LN_EPS, [b_const])
        memset(epsc[:, 1:2], RMS_EPS, [b_const])
        P.dma("sp", lngT[:], I["ln_g"].rearrange("l (c p) -> p l c", p=128), writes=[b_const])
        P.dma("sp", lnbT[:], I["ln_b"].rearrange("l (c p) -> p l c", p=128), writes=[b_const])
        P.dma("sp", adabT[:], I["ada_b"].rearrange("l (c p) -> p l c", p=128), writes=[b_const])
        P.dma("sp", pscT[:], I["pool_scale"].rearrange("l (c p) -> p l c", p=128), writes=[b_const])
        P.dma("sp", wsel[:], I["wsel"], writes=[b_const])
        P.dma("sp", qnT[:], I["mla_q_norm"].rearrange("(c p) -> p c", p=128), writes=[b_const])
        P.dma("sp", kvnT[:], I["mla_kv_norm"].rearrange("(c p) -> p c", p=128), writes=[b_const])
        for g in range(2):
            P.dma("sp", condT[:, g, :], I["cond"][g].rearrange("(c p) -> p c", p=128), writes=[b_const])
            act(scondT[:, :, g], condT[:, g, :], AF.Silu, [b_const], [b_const])

        A0 = Arena()
        adaw = [A0.bf16(8 * 512).rearrange("p (c f) -> p c f", c=8) for _ in range(2)]
        adawB = [Buf(), Buf()]
        pi = 0
        for l in range(4):
            bank = 4 + (l % 2)
            for pc in range(6):
                s = pi % 2
                pi += 1
                P.dma("pool", adaw[s], I["ada_w"][l, :, pc * 512:(pc + 1) * 512].rearrange("(c p) f -> p c f", p=128),
                      writes=[adawB[s]])
                for jj in range(4):
                    j = pc * 4 + jj
                    for k in range(8):
                        mm(psb[bank][:, 2 * j:2 * j + 2], adaw[s][:, k, jj * 128:(jj + 1) * 128], scondT[:, k, :],
                           k == 0, k == 7, [adawB[s], b_const], [psB[bank]])
            for g in range(2):
                tt(mT[:, l, :, g], psb[bank][:, 0:48].rearrange("p (j g) -> p j g", g=2)[:, :, g], adabT[:, l, :],
                   ALU.add, [psB[bank], b_const], [b_modv])
        for l in range(4):
            for g in range(2):
                sh = mT[:, l, 0:8, g]
                sc = mT[:, l, 8:16, g]
                gt = mT[:, l, 16:24, g]
                Av, Bv, Gv, AGv, ABv = (modv[:, l, g, i, :] for i in range(5))
                if l == 0:
                    ts(Av, sc, 1.0, ALU.add, [b_modv], [b_modv])
                    ts(Bv, sh, 1.0, ALU.mult, [b_modv], [b_modv])
                else:
                    stt(Av, sc, 1.0, lngT[:, l - 1, :], ALU.add, ALU.mult, [b_modv, b_const], [b_modv])
                    stt(Bv, sc, 1.0, lnbT[:, l - 1, :], ALU.add, ALU.mult, [b_modv, b_const], [b_modv])
                    tt(Bv, Bv, sh, ALU.add, [b_modv], [b_modv])
                ts(Gv, gt, 1.0, ALU.mult, [b_modv], [b_modv])
                if l < 3:
                    ts(AGv, lngT[:, l, :], ALPHA, ALU.mult, [b_const], [b_modv])
                    ts(ABv, lnbT[:, l, :], ALPHA, ALU.mult, [b_const], [b_modv])
                else:
                    ts(AGv, lngT[:, l, :], 1.0, ALU.mult, [b_const], [b_modv])
                    ts(ABv, lnbT[:, l, :], 1.0, ALU.mult, [b_const], [b_modv])

        def mv(l, g, i, c):
            return modv[:, l, g, i, c:c + 1]

        def load_x(G, xin):
            P.barrier()
            A = Arena()
            stg = [A.f32(4 * D).rearrange("p (t f) -> p t f", t=4) for _ in range(2)]
            stgB = [Buf(), Buf()]
            for blk in range(G.nblk):
                s = blk % 2
                P.dma("sp", stg[s], xin[blk * 512:(blk + 1) * 512, :].rearrange("(t p) f -> p t f", p=128),
                      writes=[stgB[s]])
                cs = slice(blk * 512, (blk + 1) * 512)
                for oc in range(8):
                    bk = mmbank()
                    for t4 in range(4):
                        tr(psb[bk][:, t4 * 128:(t4 + 1) * 128], stg[s][:, t4, oc * 128:(oc + 1) * 128], ident[:],
                           [stgB[s], b_const], [psB[bk]])
                    ts(Yp[:, oc, cs], psb[bk][:], ALPHA, ALU.mult, [psB[bk]], [YpB[oc][blk]])
                    act(H[:, oc, cs], psb[bk][:], AF.Identity, [psB[bk], b_modv], [HB[oc][blk]],
                        scale=mv(0, G.gi, 0, oc), bias=mv(0, G.gi, 1, oc))

        def outproj_epilogue(G, l, A, w_out, woB, Gsrc, GsrcB, blk, yout, inter=None):
            gi = G.gi
            cs = slice(blk * 512, (blk + 1) * 512)
            if "zb" not in A.__dict__:
                A.zb = [A.bf16(512) for _ in range(2)]
                A.zq = [A.bf16(512) for _ in range(2)]
                A.zbB = [Buf(), Buf()]
                A.zqB = [Buf(), Buf()]
                A.mean = A.f32(512)
                A.msq = A.f32(512)
                A.rstd = A.f32(512)
                A.nmr = A.f32(512)
                A.stB = Buf()
                if l == 3:
                    A.ostg = [A.f32(D) for _ in range(2)]
                    A.ostgB = [Buf(), Buf()]
            zb, zq, zbB, zqB = A.zb, A.zq, A.zbB, A.zqB
            mean, msq, rstd, nmr, stB = A.mean, A.msq, A.rstd, A.nmr, A.stB
            SUM, SQ = 4, 5
            pend = None
            for oc in range(8):
                bk = mmbank()
                for k in range(8):
                    mm(psb[bk][:], w_out[:, k, oc * 128:(oc + 1) * 128], Gsrc(k), k == 0, k == 7,
                       [woB[oc], GsrcB(k)], [psB[bk]])
                if pend is not None:
                    po, ps_ = pend
                    mm(psb[SUM][:], ones_bf[:], zb[ps_], po == 0, po == 7, [zbB[ps_], b_const], [psB[SUM]])
                    mm(psb[SQ][:], ones_bf[:], zq[ps_], po == 0, po == 7, [zqB[ps_], b_const], [psB[SQ]])
                stt(Yp[:, oc, cs], psb[bk][:], mv(l, gi, 2, oc), Yp[:, oc, cs], ALU.mult, ALU.add,
                    [psB[bk], YpB[oc][blk], b_modv], [YpB[oc][blk]])
                s = oc % 2
                act(zb[s], Yp[:, oc, cs], AF.Identity, [YpB[oc][blk]], [zbB[s]])
                act(zq[s], Yp[:, oc, cs], AF.Square, [YpB[oc][blk]], [zqB[s]])
                pend = (oc, s)
                if inter:
                    inter.pop(0)()
            po, ps_ = pend
            mm(psb[SUM][:], ones_bf[:], zb[ps_], False, True, [zbB[ps_], b_const], [psB[SUM]])
            mm(psb[SQ][:], ones_bf[:], zq[ps_], False, True, [zqB[ps_], b_const], [psB[SQ]])

            return epilogue_tail(G, l, A, blk, yout, SUM, SQ)

        def run_all(pieces):
            while pieces:
                pieces.pop(0)()

        def epilogue_tail(G, l, A, blk, yout, SUM, SQ):
            pieces = []
            pieces.append(lambda: tail_chain(G, l, A, blk, SUM, SQ))
            for oc in range(9):
                def piece(oc=oc):
                    if oc < 8:
                        tail_oc(G, l, A, blk, oc, "dve")
                    if oc >= 1:
                        tail_oc(G, l, A, blk, oc - 1, "act")
                pieces.append(piece)
            if l == 3:
                pieces.append(lambda: tail_out(G, l, A, blk, yout))
            return pieces

        def tail_chain(G, l, A, blk, SUM, SQ):
            mean, msq, rstd, nmr, stB = A.mean, A.msq, A.rstd, A.nmr, A.stB
            ts(mean, psb[SUM][:], 1.0 / D, ALU.mult, [psB[SUM]], [stB])
            tt(msq, mean, mean, ALU.mult, [stB], [stB])
            stt(msq, psb[SQ][:], 1.0 / D, msq, ALU.mult, ALU.subtract, [psB[SQ], stB], [stB])
            act(msq, msq, AF.Sqrt, [stB, b_const], [stB], bias=epsc[:, 0:1], scale=1.0)
            recip(rstd, msq, [stB], [stB])
            stt(nmr, mean, -1.0, rstd, ALU.mult, ALU.mult, [stB], [stB])

        def tail_oc(G, l, A, blk, oc, part):
            gi = G.gi
            cs = slice(blk * 512, (blk + 1) * 512)
            rstd, nmr, stB = A.rstd, A.nmr, A.stB
            if part == "dve":
                tt(Yp[:, oc, cs], Yp[:, oc, cs], rstd, ALU.mult, [YpB[oc][blk], stB], [YpB[oc][blk]])
                tt(Yp[:, oc, cs], Yp[:, oc, cs], nmr, ALU.add, [YpB[oc][blk], stB], [YpB[oc][blk]])
                return
            if l < 3:
                act(H[:, oc, cs], Yp[:, oc, cs], AF.Identity, [YpB[oc][blk], b_modv], [HB[oc][blk]],
                    scale=mv(l + 1, gi, 0, oc), bias=mv(l + 1, gi, 1, oc))
            act(Yp[:, oc, cs], Yp[:, oc, cs], AF.Identity, [YpB[oc][blk], b_modv], [YpB[oc][blk]],
                scale=mv(l, gi, 3, oc), bias=mv(l, gi, 4, oc))

        def tail_out(G, l, A, blk, yout):
            if True:
                ostg, ostgB = A.ostg, A.ostgB
                for t4 in range(4):
                    s = t4 % 2
                    for half in range(2):
                        bk = mmbank()
                        for o4 in range(4):
                            oc = half * 4 + o4
                            tr(psb[bk][:, o4 * 128:(o4 + 1) * 128],
                               Yp[:, oc, blk * 512 + t4 * 128: blk * 512 + (t4 + 1) * 128], ident[:],
                               [YpB[oc][blk], b_const], [psB[bk]])
                        P.op("act", lambda e, o=ostg[s][:, half * 512:(half + 1) * 512], i=psb[bk][:]:
                             e.activation(out=o, in_=i, func=AF.Identity), [psB[bk]], [ostgB[s]])
                    r0 = blk * 512 + t4 * 128
                    P.dma("sp", yout[r0:r0 + 128, :], ostg[s], reads=[ostgB[s]], is_out=True)

        def pool_layer(G, l, j, yout):
            gi = G.gi
            NT = G.NT
            P.barrier()
            mmpool[0] = [0, 1, 2, 3, 6, 7]
            A = Arena()
            mixed = A.bf16(8 * NT).rearrange("p (c n) -> p c n", c=8)
            mixB = [Buf() for _ in range(8)]
            mark = A.off
            pos = []
            p = 8
            for (s0, ln) in G.seqs:
                pos.append(p)
                p += ln + 8
            LB = p
            U = [A.f32(LB) for _ in range(2)]
            UB = [Buf(), Buf()]
            Sa = A.f32(LB)
            Sb = A.f32(LB)
            SB_ = Buf()
            rc = A.f32(NT)
            rcB = Buf()
            wu = [A.bf16(8 * 128).rearrange("p (c f) -> p c f", c=8) for _ in range(2)]
            wuB = [Buf(), Buf()]
            for s in range(2):
                memset(U[s], 0.0, [UB[s]])
            rsrc = I["rcnt_w"] if G.win else (I["rcnt_s"] if G.sample else I["rcnt_p"])
            for oc in range(8):
                s = oc % 2
                g = oc // 2
                P.dma("pool", wu[s], I["pool_w_in"][j, :, oc * 128:(oc + 1) * 128].rearrange("(c p) f -> p c f", p=128),
                      writes=[wuB[s]])
                if oc % 2 == 0:
                    P.dma("sp", rc, rsrc[g].partition_broadcast(128), writes=[rcB])
                for blk in range(G.nblk):
                    bk = mmbank()
                    for k in range(8):
                        mm(psb[bk][:], wu[s][:, k, :], H[:, k, blk * 512:(blk + 1) * 512], k == 0, k == 7,
                           [wuB[s], HB[k][blk]], [psB[bk]])
                    if G.sample:
                        P.op("act", lambda e, o=U[s][:, 8 + blk * 512: 8 + (blk + 1) * 512], i=psb[bk][:]:
                             e.activation(out=o, in_=i, func=AF.Identity), [psB[bk]], [UB[s]])
                    else:
                        for si in range(2):
                            P.op("act", lambda e, o=U[s][:, pos[si]:pos[si] + 256], i=psb[bk][:, si * 256:(si + 1) * 256]:
                                 e.activation(out=o, in_=i, func=AF.Identity), [psB[bk]], [UB[s]])
                u = U[s]
                tt(Sa[:, 1:LB], u[:, 0:LB - 1], u[:, 1:LB], ALU.add, [UB[s]], [SB_])
                cur = Sa
                oth = Sb
                if g >= 1:
                    tt(Sb[:, 2:LB - 1], Sa[:, 1:LB - 2], Sa[:, 3:LB], ALU.add, [SB_], [SB_])
                    cur, oth = Sb, Sa
                if g >= 2:
                    tt(Sa[:, 4:LB - 3], Sb[:, 2:LB - 5], Sb[:, 6:LB - 1], ALU.add, [SB_], [SB_])
                    cur, oth = Sa, Sb
                if g >= 3:
                    tt(Sb[:, 8:LB - 7], Sa[:, 4:LB - 11], Sa[:, 12:LB - 3], ALU.add, [SB_], [SB_])
                    cur, oth = Sb, Sa
                for si, (s0, ln) in enumerate(G.seqs):
                    p0 = pos[si]
                    tt(oth[:, p0:p0 + ln], cur[:, p0:p0 + ln], rc[:, s0:s0 + ln], ALU.mult, [SB_, rcB], [SB_])
                    tt(mixed[:, oc, s0:s0 + ln], oth[:, p0:p0 + ln], u[:, p0:p0 + ln], ALU.subtract,
                       [SB_, UB[s]], [mixB[oc]])
            if G.NT > 512:
                P.barrier()
                A.off = mark
            wz = A.bf16(8 * 1024).rearrange("p (c f) -> p c f", c=8)
            wo = A.bf16(8 * 1024).rearrange("p (c f) -> p c f", c=8)
            wg = A.bf16(4 * 2 * 256).rearrange("p (g k f) -> p g k f", g=4, k=2)
            wgB = Buf()
            P.dma("pool", wg, I["pool_w_grp"][j].rearrange("g (k p) f -> p g k f", p=128), writes=[wgB])
            wzB = load_cols(wz, I["pool_w_in"][j, :, 1024:2048], 8)
            woB = load_cols(wo, I["pool_w_out"][j], 8)
            Gt = A.bf16(8 * 512).rearrange("p (c n) -> p c n", c=8)
            GtB = [Buf() for _ in range(8)]
            st = [A.f32(512) for _ in range(2)]
            stB = [Buf(), Buf()]
            a_mark = A.off
            pend_tail = None
            mmpool[0] = [0, 1, 2, 3, 6, 7]
            for blk in range(G.nblk):
                cs = slice(blk * 512, (blk + 1) * 512)
                for oc in range(8):
                    g = oc // 2
                    jj = oc % 2
                    bz = mmbank()
                    for k in range(8):
                        mm(psb[bz][:], wz[:, k, oc * 128:(oc + 1) * 128], H[:, k, cs], k == 0, k == 7,
                           [wzB[oc], HB[k][blk]], [psB[bz]])
                    bm = mmbank()
                    for kk in range(2):
                        mm(psb[bm][:], wg[:, g, kk, jj * 128:(jj + 1) * 128], mixed[:, 2 * g + kk, cs], kk == 0, kk == 1,
                           [wgB, mixB[2 * g + kk]], [psB[bm]])
                    s = oc % 2
                    act(st[s], psb[bz][:], AF.Silu, [psB[bz]], [stB[s]])
                    stt(Gt[:, oc, :], psb[bm][:], pscT[:, j, oc:oc + 1], st[s], ALU.mult, ALU.mult,
                        [psB[bm], stB[s], b_const], [GtB[oc]])
                    if pend_tail:
                        pend_tail.pop(0)()
                if pend_tail:
                    run_all(pend_tail)
                pend_tail = outproj_epilogue(G, l, A, wo, woB, lambda k: Gt[:, k, :], lambda k: GtB[k], blk, yout)
            run_all(pend_tail)

        def mla_layer(G, l, yout):
            gi = G.gi
            NT, NK, NKL = G.NT, G.NK, G.NKL
            nkc = NK // 128
            P.barrier()
            mmpool[0] = [0, 1, 2, 3, 6, 7]
            A = Arena()
            Zs = A.bf16(8 * NT).rearrange("p (c n) -> p c n", c=8)
            ZsB = [[Buf() for _ in range(G.nblk)] for _ in range(8)]
            cqn = A.bf16(4 * NT).rearrange("p (c n) -> p c n", c=4)
            cqnB = [Buf() for _ in range(G.nblk)]
            ckvn = A.bf16(2 * NK).rearrange("p (c n) -> p c n", c=2)
            ckvnB = Buf()
            krT = A.bf16(NK)
            krTB = Buf()
            if G.sample:
                cosT = A.f32(NKL)
                sinT = A.f32(NKL)
                ropeB = Buf()
                P.dma("sp", cosT[0:64, :], I["rope_cos"], writes=[ropeB])
                P.dma("sp", sinT[0:64, :], I["rope_sin"], writes=[ropeB])
            if G.sample:
                t1 = A.f32(512)
                t2 = A.f32(512)
                tB = Buf()
            zs_mark = 4 * NT
            mark = A.off
            wa = A.bf16(8 * 832).rearrange("p (c f) -> p c f", c=8)
            waKV = Buf()
            waQ = Buf()
            P.dma("pool", wa[:, :, 512:832], I["mla_w_in"][:, 512:832].rearrange("(c p) f -> p c f", p=128), writes=[waKV])
            if G.sample:
                wkp = A.bf16(8 * 64).rearrange("p (c f) -> p c f", c=8)
                P.dma("pool", wkp, I["mla_w_krp"].rearrange("(c p) f -> p c f", p=128), writes=[waKV])
            P.dma("pool", wa[:, :, 0:512], I["mla_w_in"][:, 0:512].rearrange("(c p) f -> p c f", p=128), writes=[waQ])
            wzp = [A.bf16(8 * 128).rearrange("p (c f) -> p c f", c=8) for _ in range(2)]
            wzpB = [Buf(), Buf()]
            sq = [A.bf16(512) for _ in range(2)]
            sqB = [Buf(), Buf()]
            rsl = [A.f32(512) for _ in range(2)]
            rslB = [Buf(), Buf()]
            x32 = [A.f32(4 * 512).rearrange("p (c n) -> p c n", c=4)] * 2
            x32B = [Buf()] * 2
            rmsc = [0]
            if not G.sample:
                ck32 = A.f32(2 * 512).rearrange("p (c n) -> p c n", c=2)
                kr32 = A.f32(512)
                o32B = Buf()
                sto = [A.f32(256) for _ in range(4)]
                stoB = [Buf() for _ in range(4)]
                stk = [A.f32(64) for _ in range(4)]
                stkB = [Buf() for _ in range(4)]
            else:
                cstg = A.f32(2 * 256).rearrange("p (t f) -> p t f", t=2)
                kstg = A.f32(2 * 64).rearrange("p (t f) -> p t f", t=2)
                cstgB = Buf()

            def rms_group(banks, nfeat, normT, dst, dstB, blk, extra32=None):
                cs = slice(blk * 512, (blk + 1) * 512)
                nb = len(banks)
                ST = 4
                xs_ = x32[rmsc[0] % 2]
                xsB = x32B[rmsc[0] % 2]
                rs_ = rsl[rmsc[0] % 2]
                rsB_ = rslB[rmsc[0] % 2]
                rmsc[0] += 1
                for c, bk in enumerate(banks):
                    s = c % 2
                    act(sq[s], psb[bk][:], AF.Square, [psB[bk]], [sqB[s]])
                    act(xs_[:, c, :], psb[bk][:], AF.Identity, [psB[bk]], [xsB])
                    mm(psb[ST][:], ones_bf[:], sq[s], c == 0, c == nb - 1, [sqB[s], b_const], [psB[ST]])
                ts(rs_, psb[ST][:], 1.0 / nfeat, ALU.mult, [psB[ST]], [rsB_])
                act(rs_, rs_, AF.Sqrt, [rsB_, b_const], [rsB_], bias=epsc[:, 1:2], scale=1.0)
                recip(rs_, rs_, [rsB_], [rsB_])
                for c, bk in enumerate(banks):
                    if extra32 is not None:
                        stt(extra32[:, c, :], xs_[:, c, :], normT[:, c:c + 1], rs_, ALU.mult, ALU.mult,
                            [xsB, rsB_, b_const], [o32B])
                    stt(dst[:, c, cs], xs_[:, c, :], normT[:, c:c + 1], rs_, ALU.mult, ALU.mult,
                        [xsB, rsB_, b_const], [dstB])

            for blk in range(G.nkblk):
                cs = slice(blk * 512, (blk + 1) * 512)
                bks = [mmbank(), mmbank()]
                for c in range(2):
                    for k in range(8):
                        mm(psb[bks[c]][:], wa[:, k, 512 + c * 128:512 + (c + 1) * 128], H[:, k, cs], k == 0, k == 7,
                           [waKV, HB[k][blk]], [psB[bks[c]]])
                rms_group(bks, 256, kvnT, ckvn, ckvnB, blk, extra32=None if G.sample else ck32)
                b2 = mmbank()
                for k in range(8):
                    mm(psb[b2][0:64, :], wa[:, k, 768:832], H[:, k, cs], k == 0, k == 7, [waKV, HB[k][blk]], [psB[b2]])
                if G.sample:
                    b3 = mmbank()
                    for k in range(8):
                        mm(psb[b3][0:64, :], wkp[:, k, :], H[:, k, cs], k == 0, k == 7, [waKV, HB[k][blk]], [psB[b3]])
                    tt(t1[0:64, :], psb[b2][0:64, :], cosT[0:64, cs], ALU.mult, [psB[b2], ropeB], [tB])
                    tt(t2[0:64, :], psb[b3][0:64, :], sinT[0:64, cs], ALU.mult, [psB[b3], ropeB], [tB])
                    tt(krT[0:64, cs], t1[0:64, :], t2[0:64, :], ALU.add, [tB], [krTB])
                else:
                    act(krT[0:64, cs], psb[b2][0:64, :], AF.Identity, [psB[b2]], [krTB])
                    ts(kr32[0:64, :], psb[b2][0:64, :], 1.0, ALU.mult, [psB[b2]], [o32B])
                    for t4 in range(4):
                        s = t4
                        bk = mmbank()
                        for c in range(2):
                            tr(psb[bk][:, c * 128:(c + 1) * 128], ck32[:, c, t4 * 128:(t4 + 1) * 128], ident[:],
                               [o32B, b_const], [psB[bk]])
                        tr(psb[bk][:, 256:320], kr32[0:64, t4 * 128:(t4 + 1) * 128], ident[0:64, 0:64],
                           [o32B, b_const], [psB[bk]])
                        P.op("act", lambda e, o=sto[s], i=psb[bk][:, 0:256]: e.activation(out=o, in_=i, func=AF.Identity),
                             [psB[bk]], [stoB[s]])
                        P.op("act", lambda e, o=stk[s], i=psb[bk][:, 256:320]: e.activation(out=o, in_=i, func=AF.Identity),
                             [psB[bk]], [stkB[s]])
                        r0 = blk * 512 + t4 * 128
                        P.dma("sp", O["st_ckv"][r0:r0 + 128, :], sto[s], reads=[stoB[s]], is_out=True)
                        P.dma("sp", O["st_kr"][r0:r0 + 128, :], stk[s], reads=[stkB[s]], is_out=True)
            if G.sample:
                P.dma("sp", cstg, I["c_ckv"].rearrange("(t p) f -> p t f", p=128), writes=[cstgB])
                P.dma("sp", kstg, I["c_kr"].rearrange("(t p) f -> p t f", p=128), writes=[cstgB])
                bk = mmbank()
                for c in range(2):
                    for t2_ in range(2):
                        tr(psb[bk][:, (c * 2 + t2_) * 128:(c * 2 + t2_ + 1) * 128], cstg[:, t2_, c * 128:(c + 1) * 128],
                           ident[:], [cstgB, b_const], [psB[bk]])
                for c in range(2):
                    act(ckvn[:, c, NKL:NKL + 256], psb[bk][:, c * 256:(c + 1) * 256], AF.Identity, [psB[bk]], [ckvnB])
                bk = mmbank()
                for t2_ in range(2):
                    tr(psb[bk][0:64, t2_ * 128:(t2_ + 1) * 128], kstg[:, t2_, :], ident[:], [cstgB, b_const], [psB[bk]])
                act(krT[0:64, NKL:NKL + 256], psb[bk][0:64, 0:256], AF.Identity, [psB[bk]], [krTB])
            if G.win:
                def blend(X, XB, blk):
                    ca = slice(blk * 512, (blk + 1) * 512)
                    cb = slice((blk + 1) * 512, (blk + 2) * 512)
                    for oc in range(8):
                        ts(X[:, oc, ca], X[:, oc, ca], wsel[:, 0:1], ALU.mult, [XB[oc][blk], b_const], [XB[oc][blk]])
                        stt(X[:, oc, ca], X[:, oc, cb], wsel[:, 1:2], X[:, oc, ca], ALU.mult, ALU.add,
                            [XB[oc][blk + 1], XB[oc][blk], b_const], [XB[oc][blk]])

                for blk in range(3):
                    blend(H, HB, blk)
                P.dma("sp", cosT[0:64, 0:NT], I["rope_cos_w"], reads=[krTB], writes=[ropeB])
                P.dma("sp", sinT[0:64, 0:NT], I["rope_sin_w"], reads=[krTB], writes=[ropeB])
            for blk in range(G.nblk):
                cs = slice(blk * 512, (blk + 1) * 512)
                banks = [mmbank() for _ in range(4)]
                for c in range(4):
                    for k in range(8):
                        mm(psb[banks[c]][:], wa[:, k, c * 128:(c + 1) * 128], H[:, k, cs], k == 0, k == 7,
                           [waQ, HB[k][blk]], [psB[banks[c]]])
                rms_group(banks, 512, qnT, cqn, cqnB[blk], blk)
            if G.win:
                for blk in range(3):
                    blend(Yp, YpB, blk)
            for oc in range(8):
                s = oc % 2
                P.dma("pool", wzp[s], I["mla_w_in"][:, 832 + oc * 128:832 + (oc + 1) * 128].rearrange("(c p) f -> p c f", p=128),
                      writes=[wzpB[s]])
                for blk in range(G.nblk):
                    cs = slice(blk * 512, (blk + 1) * 512)
                    bk = mmbank()
                    for k in range(8):
                        mm(psb[bk][:], wzp[s][:, k, :], H[:, k, cs], k == 0, k == 7, [wzpB[s], HB[k][blk]], [psB[bk]])
                    act(Zs[:, oc, cs], psb[bk][:], AF.Silu, [psB[bk]], [ZsB[oc][blk]])

            small = G.NT <= 512
            if not small:
                P.barrier()
                A.off = mark
            mmpool[0] = [0, 1, 2, 3]
            wq_l = [A.bf16(4 * 256).rearrange("p (c f) -> p c f", c=4) for _ in range(2)]
            wkv_l = [A.bf16(2 * 256).rearrange("p (c f) -> p c f", c=2) for _ in range(2)]
            whB_l = [Buf(), Buf()]
            Hf = H[:].rearrange("p c n -> p (c n)")
            ho = [0]

            def hb(n):
                if small:
                    return A.bf16(n)
                o = ho[0]
                ho[0] += n
                assert ho[0] <= 8 * 2048
                return Hf[:, o:o + n]

            qn_l = [hb(NT) for _ in range(2)]
            qr_l = [hb(NT) for _ in range(2)]
            kn_l = [hb(NK) for _ in range(2)]
            Vh_l = [hb(nkc * 128).rearrange("p (k f) -> p k f", k=nkc) for _ in range(2)]
            hB_l = [[Buf() for _ in range(4)] for _ in range(2)]
            pt = [A.bf16(512) for _ in range(3)]
            ptB = [Buf() for _ in range(3)]
            rD = A.f32(512)
            tO = A.f32(512)
            rB = Buf()
            if G.sample:
                qblocks = [(b * 512, 512, list(range(nkc)), b) for b in range(G.nblk)]
            else:
                qblocks = [(s * 256, 256, [2 * s, 2 * s + 1], 0) for s in range(2)]
            ptc = 0
            accset = 0
            def emit_proj(h):
                s2 = h % 2
                wq, wkv, whB = wq_l[s2], wkv_l[s2], whB_l[s2]
                qn, qr, kn, Vh = qn_l[s2], qr_l[s2], kn_l[s2], Vh_l[s2]
                qnB, qrB, knB, VhB = hB_l[s2]
                P.dma("pool", wq, I["mla_wq_pack"][h].rearrange("(c p) f -> p c f", p=128), writes=[whB])
                P.dma("pool", wkv, I["mla_w_ukv"][:, h * 256:(h + 1) * 256].rearrange("(c p) f -> p c f", p=128), writes=[whB])
                for blk in range(G.nblk):
                    cs = slice(blk * 512, (blk + 1) * 512)
                    bk = mmbank()
                    for c in range(4):
                        mm(psb[bk][:], wq[:, c, 0:128], cqn[:, c, cs], c == 0, c == 3, [whB, cqnB[blk]], [psB[bk]])
                    act(qn[:, cs], psb[bk][:], AF.Identity, [psB[bk]], [qnB], scale=MLA_SCALE)
                    bk = mmbank()
                    for c in range(4):
                        mm(psb[bk][0:64, :], wq[:, c, 128:192], cqn[:, c, cs], c == 0, c == 3, [whB, cqnB[blk]], [psB[bk]])
                    if G.sample:
                        bk2 = mmbank()
                        for c in range(4):
                            mm(psb[bk2][0:64, :], wq[:, c, 192:256], cqn[:, c, cs], c == 0, c == 3, [whB, cqnB[blk]], [psB[bk2]])
                        stt(t1[0:64, :], psb[bk][0:64, :], MLA_SCALE, cosT[0:64, cs], ALU.mult, ALU.mult, [psB[bk], ropeB], [tB])
                        stt(t2[0:64, :], psb[bk2][0:64, :], MLA_SCALE, sinT[0:64, cs], ALU.mult, ALU.mult, [psB[bk2], ropeB], [tB])
                        tt(qr[0:64, cs], t1[0:64, :], t2[0:64, :], ALU.add, [tB], [qrB])
                    else:
                        act(qr[0:64, cs], psb[bk][0:64, :], AF.Identity, [psB[bk]], [qrB], scale=MLA_SCALE)
                for kb in range(0, NK, 512):
                    w = min(512, NK - kb)
                    bk = mmbank()
                    for c in range(2):
                        mm(psb[bk][:, 0:w], wkv[:, c, 0:128], ckvn[:, c, kb:kb + w], c == 0, c == 1, [whB, ckvnB], [psB[bk]])
                    act(kn[:, kb:kb + w], psb[bk][:, 0:w], AF.Identity, [psB[bk]], [knB])
                for k0 in range(0, nkc, 4):
                    n4 = min(4, nkc - k0)
                    bk = mmbank()
                    for i4 in range(n4):
                        kc = k0 + i4
                        for c in range(2):
                            mm(psb[bk][:, i4 * 128:(i4 + 1) * 128], ckvn[:, c, kc * 128:(kc + 1) * 128], wkv[:, c, 128:256],
                               c == 0, c == 1, [whB, ckvnB], [psB[bk]])
                    P.op("dve", lambda e, o=Vh[:, k0:k0 + n4, :].rearrange("p k f -> p (k f)"), i=psb[bk][:, 0:n4 * 128]:
                         e.tensor_copy(out=o, in_=i), [psB[bk]], [VhB])
            def emit_attn(h, inject):
                nonlocal ptc, accset
                s2 = h % 2
                qn, qr, kn, Vh = qn_l[s2], qr_l[s2], kn_l[s2], Vh_l[s2]
                qnB, qrB, knB, VhB = hB_l[s2]
                for qi, (q0, nq, kcs, zblk) in enumerate(qblocks):
                    if qi == 1 and inject is not None:
                        inject()
                    qs = slice(q0, q0 + nq)
                    OB, DB = (4, 5) if accset % 2 == 0 else (6, 7)
                    accset += 1
                    sbanks = {}

                    def qk_a(kc):
                        bk = mmbank()
                        sbanks[kc] = bk
                        mm(psb[bk][:, 0:nq], kn[:, kc * 128:(kc + 1) * 128], qn[:, qs], True, False, [knB, qnB], [psB[bk]])

                    def qk_b(kc):
                        bk = sbanks[kc]
                        mm(psb[bk][:, 0:nq], krT[0:64, kc * 128:(kc + 1) * 128], qr[0:64, qs], False, True,
                           [krTB, qrB], [psB[bk]])

                    qk_a(kcs[0])
                    qk_b(kcs[0])
                    for ii, kc in enumerate(kcs):
                        nxt = kcs[ii + 1] if ii + 1 < len(kcs) else None
                        if nxt is not None:
                            qk_a(nxt)
                        bk = sbanks[kc]
                        pi_ = ptc % 3
                        ptc += 1
                        act(pt[pi_][:, 0:nq], psb[bk][:, 0:nq], AF.Exp, [psB[bk]], [ptB[pi_]])
                        mm(psb[OB][:, 0:nq], Vh[:, kc, :], pt[pi_][:, 0:nq], ii == 0, ii == len(kcs) - 1,
                           [VhB, ptB[pi_]], [psB[OB]])
                        if nxt is not None:
                            qk_b(nxt)
                        mm(psb[DB][:, 0:nq], ones_bf[:], pt[pi_][:, 0:nq], ii == 0, ii == len(kcs) - 1,
                           [b_const, ptB[pi_]], [psB[DB]])
                    recip(rD[:, 0:nq], psb[DB][:, 0:nq], [psB[DB]], [rB])
                    tt(tO[:, 0:nq], psb[OB][:, 0:nq], rD[:, 0:nq], ALU.mult, [psB[OB], rB], [rB])
                    tt(Zs[:, h, qs], tO[:, 0:nq], Zs[:, h, qs], ALU.mult, [rB, ZsB[h][zblk]], [ZsB[h][zblk]])

            emit_proj(0)
            for h in range(8):
                emit_attn(h, (lambda h=h: emit_proj(h + 1)) if h < 7 else None)
            if not small:
                P.barrier()
                A.off = zs_mark
            mmpool[0] = [0, 1, 2, 3, 6, 7]
            wo = A.bf16(8 * 1024).rearrange("p (c f) -> p c f", c=8)
            woB = load_cols(wo, I["mla_w_out"], 8)
            pend_tail = None
            for blk in range(G.nblk):
                t_ = outproj_epilogue(G, l, A, wo, woB, lambda k, blk=blk: Zs[:, k, blk * 512:(blk + 1) * 512],
                                      lambda k, blk=blk: ZsB[k][blk], blk, yout, inter=pend_tail)
                if pend_tail:
                    run_all(pend_tail)
                pend_tail = t_
            run_all(pend_tail)

        def na_layer(G, l, yout):
            gi = G.gi
            NT, NK = G.NT, G.NK
            nkc = NK // 128
            P.barrier()
            mmpool[0] = [0, 1, 2, 3]
            A = Arena()
            Zs = A.bf16(8 * NT).rearrange("p (c n) -> p c n", c=8)
            ZsB = [[Buf() for _ in range(G.nblk)] for _ in range(8)]
            wp = [A.bf16(8 * 512).rearrange("p (c f) -> p c f", c=8) for _ in range(2)]
            wpB = [Buf(), Buf()]
            nkc = NT // 128 + 2
            NKA = nkc * 128
            qh_l = [A.bf16(NT) for _ in range(2)]
            kh_l = [A.bf16(NKA) for _ in range(2)]
            Vh_l = [A.bf16(nkc * 128).rearrange("p (k f) -> p k f", k=nkc) for _ in range(2)]
            qkvB_l = [[Buf(), Buf(), Buf()] for _ in range(2)]
            pt = [A.bf16(256) for _ in range(6)]
            ptB = [Buf() for _ in range(6)]
            rD = A.f32(256)
            tO = A.f32(256)
            rB = Buf()
            if G.sample:
                tabs_l = [[[A.bf16(NTAB * 64) for _ in range(2)] for _ in range(2)] for _ in range(2)]
                tabB_l = [[Buf(), Buf()] for _ in range(2)]
                kstg = A.f32(2 * 128).rearrange("p (t f) -> p t f", t=2)
                kstgB = Buf()
            else:
                k32 = A.f32(512)
                k32B = Buf()
                kst = [A.f32(128) for _ in range(4)]
                kstB = [Buf() for _ in range(4)]
                vst = [A.f32(128) for _ in range(4)]
                vstB = [Buf() for _ in range(4)]
            ptc = 0
            sbc = 0
            accset = 0
            def emit_proj(hg, part):
                s = hg % 2
                qh, kh, Vh = qh_l[s], kh_l[s], Vh_l[s]
                qB, kB, VB = qkvB_l[s]
                if G.sample:
                    tabs, tabB = tabs_l[s], tabB_l[s]
                w = wp[s]
                if part < 0:
                    P.dma("pool", wp[s], I["na_w_pack"][hg].rearrange("(c p) f -> p c f", p=128), writes=[wpB[s]])
                    if G.sample:
                        for hp in range(2):
                            for kind in range(2):
                                P.dma("pool", tabs[hp][kind], I["na_tab"][kind, 2 * hg + hp], writes=[tabB[hp]])
                        P.dma("sp", kstg, I["c_nak"][:, hg * 128:(hg + 1) * 128].rearrange("(t p) f -> p t f", p=128),
                              writes=[kstgB])
                        P.dma("pool", Vh[:, NT // 128:NT // 128 + 2, :], I["c_nav"][:, hg * 128:(hg + 1) * 128].rearrange("(t p) f -> p t f", p=128),
                              writes=[VB])
                        bk = mmbank()
                        for t2_ in range(2):
                            tr(psb[bk][:, t2_ * 128:(t2_ + 1) * 128], kstg[:, t2_, :], ident[:], [kstgB, b_const], [psB[bk]])
                        act(kh[:, NT:NT + 256], psb[bk][:, 0:256], AF.Identity, [psB[bk]], [kB])
                    return
                blk = part
                cs = slice(blk * 512, (blk + 1) * 512)
                bk = mmbank()
                for k in range(8):
                    mm(psb[bk][:], w[:, k, 0:128], H[:, k, cs], k == 0, k == 7, [wpB[s], HB[k][blk]], [psB[bk]])
                act(qh[:, cs], psb[bk][:], AF.Identity, [psB[bk]], [qB], scale=NA_SCALE)
                bk = mmbank()
                for k in range(8):
                    mm(psb[bk][:], w[:, k, 128:256], H[:, k, cs], k == 0, k == 7, [wpB[s], HB[k][blk]], [psB[bk]])
                act(kh[:, cs], psb[bk][:], AF.Identity, [psB[bk]], [kB])
                if not G.sample:
                    ts(k32, psb[bk][:], 1.0, ALU.mult, [psB[bk]], [k32B])
                    for t4 in range(4):
                        s2 = t4
                        bk2 = mmbank()
                        tr(psb[bk2][:, 0:128], k32[:, t4 * 128:(t4 + 1) * 128], ident[:], [k32B, b_const], [psB[bk2]])
                        P.op("act", lambda e, o=kst[s2], i=psb[bk2][:, 0:128]: e.activation(out=o, in_=i, func=AF.Identity),
                             [psB[bk2]], [kstB[s2]])
                        r0 = blk * 512 + t4 * 128
                        P.dma("sp", O["st_k"][r0:r0 + 128, hg * 128:(hg + 1) * 128], kst[s2], reads=[kstB[s2]], is_out=True)
                bk = mmbank()
                for t4 in range(4):
                    for k in range(8):
                        mm(psb[bk][:, t4 * 128:(t4 + 1) * 128], H[:, k, blk * 512 + t4 * 128: blk * 512 + (t4 + 1) * 128],
                           w[:, k, 256:384], k == 0, k == 7, [wpB[s], HB[k][blk]], [psB[bk]])
                P.op("dve", lambda e, o=Vh[:, blk * 4:(blk + 1) * 4, :].rearrange("p k f -> p (k f)"), i=psb[bk][:]:
                     e.tensor_copy(out=o, in_=i), [psB[bk]], [VB])
                if not G.sample:
                    for t4 in range(4):
                        s2 = t4
                        P.op("act", lambda e, o=vst[s2], i=psb[bk][:, t4 * 128:(t4 + 1) * 128]:
                             e.activation(out=o, in_=i, func=AF.Identity), [psB[bk]], [vstB[s2]])
                        r0 = blk * 512 + t4 * 128
                        P.dma("sp", O["st_v"][r0:r0 + 128, hg * 128:(hg + 1) * 128], vst[s2], reads=[vstB[s2]], is_out=True)
                bk = mmbank()
                for k in range(8):
                    mm(psb[bk][:], w[:, k, 384:512], H[:, k, cs], k == 0, k == 7, [wpB[s], HB[k][blk]], [psB[bk]])
                act(Zs[:, hg, cs], psb[bk][:], AF.Silu, [psB[bk]], [ZsB[hg][blk]])
            def emit_attn(hg, inject):
                nonlocal ptc
                s = hg % 2
                qh, kh, Vh = qh_l[s], kh_l[s], Vh_l[s]
                qB, kB, VB = qkvB_l[s]
                if G.sample:
                    tabs, tabB = tabs_l[s], tabB_l[s]
                items = []
                if G.sample:
                    ng = NT // 256
                    nkl = NT // 128
                    for g8 in range(ng):
                        if g8 == 0:
                            loc = [(kc, 1, 6 - 2 * kc) for kc in range(4)]
                        elif g8 == ng - 1:
                            loc = [(kc, 1, 6 - (2 * kc - 4 * g8)) for kc in range(2 * g8 - 2, 2 * g8 + 2)]
                        else:
                            loc = [(2 * g8 - 2 + jx, 0, 6 - (2 * jx - 4)) for jx in range(6)]
                        items.append((g8 * 256, 256, loc + [(nkl, None, None), (nkl + 1, None, None)], g8 // 2))
                else:
                    for s_ in range(2):
                        items.append((s_ * 256, 256, [(2 * s_, None, None), (2 * s_ + 1, None, None)], 0))
                steps = []
                for it_i, itm in enumerate(items):
                    kcl = itm[2]
                    for hp in range(2):
                        for b0 in range(0, len(kcl), 2):
                            steps.append((it_i, hp, kcl[b0:b0 + 2], b0))
                LA = 3

                def emit_qk(t):
                    it_i, hp, ents, b0 = steps[t]
                    q0, nq, kcl, zblk = items[it_i]
                    pr = slice(hp * 64, hp * 64 + 64)
                    bk = t % 4
                    for e_i, ent in enumerate(ents):
                        kc, kind, bb0 = ent
                        dst = psb[bk][:, e_i * 256:e_i * 256 + nq]
                        mm(dst, kh[pr, kc * 128:(kc + 1) * 128], qh[pr, q0:q0 + nq], e_i == 0, kind is None, [kB, qB], [psB[bk]])
                    for e_i, ent in enumerate(ents):
                        kc, kind, bb0 = ent
                        dst = psb[bk][:, e_i * 256:e_i * 256 + nq]
                        if kind is not None:
                            mm(dst, ident_bf[:], tabs[hp][kind][:, bb0 * 64: bb0 * 64 + 256], False, True,
                               [b_const, tabB[hp]], [psB[bk]])

                def emit_rest(t):
                    nonlocal ptc
                    it_i, hp, ents, b0 = steps[t]
                    q0, nq, kcl, zblk = items[it_i]
                    pr = slice(hp * 64, hp * 64 + 64)
                    qs = slice(q0, q0 + nq)
                    bk = t % 4
                    OB, DB = (4, 5) if it_i % 2 == 0 else (6, 7)
                    pts = []
                    for e_i, ent in enumerate(ents):
                        pi_ = ptc % 6
                        ptc += 1
                        act(pt[pi_][:, 0:nq], psb[bk][:, e_i * 256:e_i * 256 + nq], AF.Exp, [psB[bk]], [ptB[pi_]])
                        pts.append(pi_)
                    for e_i, ent in enumerate(ents):
                        kc = ent[0]
                        ci = b0 + e_i
                        pi_ = pts[e_i]
                        mm(psb[OB][pr, 0:nq], Vh[:, kc, pr], pt[pi_][:, 0:nq], ci == 0, ci == len(kcl) - 1,
                           [VB, ptB[pi_]], [psB[OB]])
                        mm(psb[DB][pr, 0:nq], ones_bf[:, pr], pt[pi_][:, 0:nq], ci == 0, ci == len(kcl) - 1,
                           [b_const, ptB[pi_]], [psB[DB]])
                    if hp == 1 and b0 + 2 >= len(kcl):
                        recip(rD[:, 0:nq], psb[DB][:, 0:nq], [psB[DB]], [rB])
                        tt(tO[:, 0:nq], psb[OB][:, 0:nq], rD[:, 0:nq], ALU.mult, [psB[OB], rB], [rB])
                        tt(Zs[:, hg, qs], tO[:, 0:nq], Zs[:, hg, qs], ALU.mult, [rB, ZsB[hg][zblk]], [ZsB[hg][zblk]])

                inj = list(inject) if inject else []
                npc = len(inj)
                nr = 0
                for t in range(len(steps)):
                    if inj and t >= (npc - len(inj) + 1) * len(steps) // (npc + 1):
                        while nr < t:
                            emit_rest(nr)
                            nr += 1
                        inj.pop(0)()
                    emit_qk(t)
                    while nr <= t - LA:
                        emit_rest(nr)
                        nr += 1
                while nr < len(steps):
                    emit_rest(nr)
                    nr += 1
                while inj:
                    inj.pop(0)()

            for hg in range(8):
                for part in range(-1, G.nblk):
                    emit_proj(hg, part)
                emit_attn(hg, None)
            if G.NT > 512:
                P.barrier()
                A.off = 4 * NT
            mmpool[0] = [0, 1, 2, 3, 6, 7]
            wo = A.bf16(8 * 1024).rearrange("p (c f) -> p c f", c=8)
            woB = load_cols(wo, I["na_w_out"], 8)
            pend_tail = None
            for blk in range(G.nblk):
                t_ = outproj_epilogue(G, l, A, wo, woB, lambda k, blk=blk: Zs[:, k, blk * 512:(blk + 1) * 512],
                                      lambda k, blk=blk: ZsB[k][blk], blk, yout, inter=pend_tail)
                if pend_tail:
                    run_all(pend_tail)
                pend_tail = t_
            run_all(pend_tail)

        GP = Grp(0, 512, [(0, 256), (256, 256)], False)
        GS = Grp(1, 2048, [(0, 2048)], True)
        GW = Grp(1, 1536, [(0, 1536)], True, NKL=2048, win=True)
        stage = [0]

        def go():
            stage[0] += 1
            return stage[0] <= stop

        for G, G2, xin, yout in ((GP, GP, I["xp"], O["yp"]), (GS, GW, I["xs"], O["ys"])):
            if go():
                load_x(G, xin)
            if go():
                pool_layer(G, 0, 0, yout)
            if go():
                mla_layer(G2, 1, yout)
            if go():
                na_layer(G2, 2, yout)
            if go():
                pool_layer(G2, 3, 1, yout)
        if dbg:
            P.barrier()
            dbg_out = nc.dram_tensor("dbg", [128, 8 * 512 + 4 * 2 * 5 * 8], F32, kind="ExternalOutput").ap()
            P.dma("sp", dbg_out[:, 0:4096].rearrange("p (c n) -> p c n", c=8), Yp[:, :, 0:512], is_out=True)
            P.dma("sp", dbg_out[:, 4096:4096 + 320], modv[:].rearrange("p l g i c -> p (l g i c)"), is_out=True)
        P.emit()
        print("pe marks", P.marks)
        print("instr counts", {q: len(P.lists[q]) for q in P.QS}, "max sem", {q: max([i.val or 0 for i in P.lists[q]] + [0]) for q in P.QS})
    return nc


def _rope_tables():
    nf = 16
    inv = (10000.0 ** (-np.arange(nf, dtype=np.float32) / nf)).astype(np.float32)
    t = np.arange(2048)
    pos = np.stack([t // GRID_W, t % GRID_W], -1).astype(np.float32)
    ang = (pos[:, :, None] * inv).astype(np.float32)
    cos = np.cos(ang).astype(np.float32)
    sin = np.sin(ang).astype(np.float32)
    C = np.zeros((64, 2048), np.float32)
    S = np.zeros((64, 2048), np.float32)
    for a in range(2):
        for s in range(2):
            for i in range(nf):
                f = a * 32 + s * 16 + i
                C[f] = cos[:, a, i]
                S[f] = sin[:, a, i] * (-1.0 if s == 0 else 1.0)
    perm = np.array([a * 32 + (1 - s) * 16 + i for a in range(2) for s in range(2) for i in range(nf)])
    return C, S, perm


def _na_table_index():
    idx = np.full((2, 128, NTAB, 64), 15 * 31, np.int64)
    qc = np.arange(64)
    cstart = np.clip(qc - 8, 0, 48)
    for kind in range(2):
        for krl in range(2):
            for kc in range(64):
                p = krl * 64 + kc
                colv = (kc >= cstart) & (kc < cstart + 16)
                for bb in range(NTAB):
                    dr = 6 - bb + krl
                    if dr < -7 or dr > 7:
                        continue
                    if kind == 0 and (dr < -4 or dr > 3):
                        continue
                    dc = kc - qc
                    v = (dr + 7) * 31 + (dc + 15)
                    idx[kind, p, bb, :] = np.where(colv, v, 15 * 31)
    return idx


def _rcnt(L, reps):
    t = np.arange(L)
    out = np.zeros((4, L), np.float32)
    for g, w in enumerate(POOL_WINDOWS):
        lo = np.clip(t - w // 2, 0, L)
        hi = np.clip(t + w - w // 2, 0, L)
        out[g] = (1.0 / (hi - lo).astype(np.float32)).astype(np.float32)
    return np.tile(out, (1, reps))


_NC_CACHE = {}


def kernel(x_prompt, x_sample, cache_mla_ckv, cache_mla_krope, cache_na_k, cache_na_v, c, c_ctx,
           ada_w, ada_b, ln_g, ln_b, pool_w_in, pool_w_grp, pool_scale, pool_w_out,
           mla_w_in, mla_q_norm, mla_w_uq, mla_kv_norm, mla_w_ukv, mla_w_out,
           na_w_in, na_rpb, na_w_out):
    f = lambda a: np.ascontiguousarray(np.asarray(a, dtype=np.float32))
    x_prompt, x_sample = f(x_prompt), f(x_sample)
    C, S, perm = _rope_tables()
    mla_w_in = f(mla_w_in)[0]
    mla_w_uq = f(mla_w_uq)[0]
    wq_pack = np.zeros((8, 512, 256), np.float32)
    for h in range(8):
        blk = mla_w_uq[:, h * 192:(h + 1) * 192]
        wq_pack[h, :, 0:192] = blk
        wq_pack[h, :, 192:256] = blk[:, 128:192][:, perm]
    w_krp = np.ascontiguousarray(mla_w_in[:, 768:832][:, perm])
    na_w = f(na_w_in)[0]
    na_pack = np.stack([np.concatenate([na_w[:, p * 1024 + hg * 128: p * 1024 + (hg + 1) * 128] for p in range(4)], 1)
                        for hg in range(8)], 0)
    rpb = f(na_rpb)[0]
    rpb_ext = np.concatenate([rpb.reshape(16, -1), np.full((16, 1), NEG, np.float32)], 1)
    idx = _na_table_index()
    na_tab = np.ascontiguousarray(rpb_ext[:, idx].transpose(1, 0, 2, 3, 4).reshape(2, 16, 128, NTAB * 64))
    rcs = _rcnt(2048, 1)
    shared = {
        "ada_w": f(ada_w), "ada_b": f(ada_b), "ln_g": f(ln_g), "ln_b": f(ln_b),
        "pool_w_in": f(pool_w_in), "pool_w_grp": f(pool_w_grp), "pool_scale": f(pool_scale), "pool_w_out": f(pool_w_out),
        "mla_w_in": mla_w_in, "mla_w_krp": w_krp, "mla_q_norm": f(mla_q_norm)[0], "mla_wq_pack": wq_pack,
        "mla_kv_norm": f(mla_kv_norm)[0], "mla_w_ukv": f(mla_w_ukv)[0], "mla_w_out": f(mla_w_out)[0],
        "na_w_pack": np.ascontiguousarray(na_pack), "na_w_out": f(na_w_out)[0], "na_tab": na_tab,
        "rope_cos": C, "rope_sin": S, "rcnt_p": _rcnt(256, 2), "rcnt_s": rcs,
        "ident": np.eye(128, dtype=np.float32),
    }
    c, c_ctx = f(c), f(c_ctx)
    in_maps = []
    wsel = [np.tile(np.array([[1.0, 0.0]], np.float32), (128, 1)), np.tile(np.array([[0.0, 1.0]], np.float32), (128, 1))]
    for core in range(8):
        b = core // 2
        m = dict(shared)
        m["xp"] = x_prompt[2 * core:2 * core + 2].reshape(512, D)
        m["xs"] = x_sample[b]
        m["cond"] = np.stack([c_ctx, c[b]], 0)
        m["c_ckv"] = f(cache_mla_ckv)[b, 0]
        m["c_kr"] = f(cache_mla_krope)[b, 0]
        m["c_nak"] = f(cache_na_k)[b, 0].reshape(256, D)
        m["c_nav"] = f(cache_na_v)[b, 0].reshape(256, D)
        w0 = 512 * (core % 2)
        m["wsel"] = wsel[core % 2]
        m["rcnt_w"] = np.ascontiguousarray(rcs[:, w0:w0 + 1536])
        m["rope_cos_w"] = np.ascontiguousarray(C[:, w0:w0 + 1536])
        m["rope_sin_w"] = np.ascontiguousarray(S[:, w0:w0 + 1536])
        in_maps.append(m)
    if "nc" not in _NC_CACHE:
        _NC_CACHE["nc"] = build_program()
    res = run_bass_kernel_spmd(_NC_CACHE["nc"], in_maps, core_ids=list(range(8)))
    R = res.results
    yp = np.concatenate([R[i]["yp"].reshape(2, 256, D) for i in range(8)], 0)
    ys = np.stack([np.concatenate([R[2 * b]["ys"][0:1024], R[2 * b + 1]["ys"][512:1536]], 0) for b in range(4)], 0)
    st_ckv = np.concatenate([R[i]["st_ckv"].reshape(2, 1, 256, 256) for i in range(8)], 0)
    st_kr = np.concatenate([R[i]["st_kr"].reshape(2, 1, 256, 64) for i in range(8)], 0)
    st_k = np.concatenate([R[i]["st_k"].reshape(2, 1, 256, 16, 64) for i in range(8)], 0)
    st_v = np.concatenate([R[i]["st_v"].reshape(2, 1, 256, 16, 64) for i in range(8)], 0)
    return (yp.astype(np.float32), ys.astype(np.float32), st_ckv.astype(np.float32), st_kr.astype(np.float32),
            st_k.astype(np.float32), st_v.astype(np.float32))
```

```python
import numpy as np
import concourse.bass as bass
import concourse.mybir as mybir
from concourse.bass_utils import run_bass_kernel_spmd
from contextlib import ExitStack

F32 = mybir.dt.float32
BF16 = mybir.dt.bfloat16
AF = mybir.ActivationFunctionType
ALU = mybir.AluOpType

D = 1024
DEPTH = 4
GRID_W = 64
ALPHA = (2 * DEPTH) ** 0.25
LN_EPS = 1e-5
RMS_EPS = 1e-6
MLA_SCALE = (128 + 64) ** -0.5
NA_SCALE = 64 ** -0.5
POOL_WINDOWS = (2, 4, 8, 16)
NEG = -1e30
NTAB = 14


class Buf:
    __slots__ = ("name", "w", "rs", "excl")

    def __init__(self, name="", excl=False):
        self.name = name
        self.w = None
        self.rs = []
        self.excl = excl


class Ins:
    __slots__ = ("q", "fn", "deps", "inc", "sem", "val", "isdma")


class Prog:
    QS = ("pe", "act", "dve", "pool", "sp")
    NDMA = 8

    def __init__(self, nc, es):
        self.nc = nc
        self.lists = {q: [] for q in self.QS}
        self.sems = {q: es.enter_context(nc.semaphore("s_" + q)) for q in ("pe", "act", "dve", "pool")}
        self.dsems = {q: [es.enter_context(nc.semaphore(f"d_{q}{i}")) for i in range(self.NDMA)]
                      for q in ("sp", "pool")}
        self.dcount = {q: 0 for q in self.dsems}
        self.dlast = {q: [None] * self.NDMA for q in self.dsems}
        self.out_dmas = []
        self.fence = []

    def barrier(self):
        self.marks = getattr(self, "marks", [])
        self.marks.append(len(self.lists["pe"]))
        f = []
        for q in self.QS:
            for ins in reversed(self.lists[q]):
                if not ins.isdma:
                    f.append(ins)
                    break
        for q in self.dlast:
            f.extend(i for i in self.dlast[q] if i is not None)
        self.fence = f

    def _add(self, q, fn, reads, writes, isdma=False):
        ins = Ins()
        ins.q = q; ins.fn = fn; ins.deps = []; ins.inc = False
        ins.isdma = isdma; ins.val = None
        writes = writes + [b for b in reads if b.excl and b not in writes]
        reads = [b for b in reads if not b.excl]
        deps = list(self.fence)
        for b in reads:
            if b.w is not None:
                deps.append(b.w)
        for b in writes:
            if b.w is not None:
                deps.append(b.w)
            deps.extend(b.rs)
        if isdma:
            k = self.dcount[q] % self.NDMA
            self.dcount[q] += 1
            prev = self.dlast[q][k]
            if prev is not None:
                deps.append(prev)
            self.dlast[q][k] = ins
            ins.sem = ("d", q, k)
            ins.inc = True
        else:
            ins.sem = ("c", q)
        seen = set()
        for d in deps:
            if d is ins or id(d) in seen:
                continue
            seen.add(id(d))
            if (not d.isdma) and (not isdma) and d.q == "pe" and q == "pe":
                continue
            d.inc = True
            ins.deps.append(d)
        for b in reads:
            if not isdma:
                b.rs = [r for r in b.rs if r.isdma or r.q != q]
            b.rs.append(ins)
        for b in writes:
            b.w = ins
            b.rs = []
        self.lists[q].append(ins)
        return ins

    def op(self, q, fn, reads=(), writes=()):
        return self._add(q, fn, list(reads), list(writes))

    def dma(self, q, out, in_, reads=(), writes=(), is_out=False):
        ins = self._add(q, lambda e: e.dma_start(out=out, in_=in_), list(reads), list(writes), isdma=True)
        if is_out:
            self.out_dmas.append(ins)
        return ins

    def emit(self):
        nc = self.nc
        for q in self.QS:
            c = 0
            dc = {}
            for ins in self.lists[q]:
                if ins.isdma:
                    dc[ins.sem] = dc.get(ins.sem, 0) + 16
                    ins.val = dc[ins.sem]
                elif ins.inc:
                    c += 1
                    ins.val = c

        def semobj(key):
            return self.sems[key[1]] if key[0] == "c" else self.dsems[key[1]][key[2]]

        def run(q, eng):
            waited = {}
            for ins in self.lists[q]:
                need = {}
                for d in ins.deps:
                    if d.val > need.get(d.sem, 0):
                        need[d.sem] = d.val
                for k, v in need.items():
                    if waited.get(k, 0) >= v:
                        continue
                    eng.wait_ge(semobj(k), v)
                    waited[k] = v
                r = ins.fn(eng)
                if ins.inc:
                    r.then_inc(semobj(ins.sem), 16 if ins.isdma else 1)
            if q == "sp":
                fin = {}
                for o in self.out_dmas:
                    if o.val > fin.get(o.sem, 0):
                        fin[o.sem] = o.val
                for k, v in fin.items():
                    eng.wait_ge(semobj(k), v)

        with nc.allow_non_contiguous_dma(reason="tiny transposed parameter vectors"), nc.Block() as block:
            @block.tensor
            def _(e):
                run("pe", e)

            @block.scalar
            def _(e):
                run("act", e)

            @block.vector
            def _(e):
                run("dve", e)

            @block.gpsimd
            def _(e):
                run("pool", e)

            @block.sync
            def _(e):
                run("sp", e)


class Grp:
    def __init__(self, gi, NT, seqs, sample, NKL=None, win=False):
        self.gi = gi
        self.NT = NT
        self.nblk = NT // 512
        self.seqs = seqs
        self.sample = sample
        self.win = win
        self.NKL = NT if NKL is None else NKL
        self.nkblk = self.NKL // 512
        self.NK = self.NKL + (256 if sample else 0)


def build_program(stop=99, dbg=False):
    nc = bass.Bass("TRN2", target_bir_lowering=False)

    def din(name, shape):
        return nc.dram_tensor(name, list(shape), F32, kind="ExternalInput").ap()

    def dout(name, shape):
        return nc.dram_tensor(name, list(shape), F32, kind="ExternalOutput").ap()

    I = {}
    for name, shape in [
        ("xp", (512, D)), ("xs", (2048, D)), ("cond", (2, D)),
        ("c_ckv", (256, 256)), ("c_kr", (256, 64)), ("c_nak", (256, D)), ("c_nav", (256, D)),
        ("ada_w", (4, D, 3 * D)), ("ada_b", (4, 3 * D)), ("ln_g", (4, D)), ("ln_b", (4, D)),
        ("pool_w_in", (2, D, 2 * D)), ("pool_w_grp", (2, 4, 256, 256)), ("pool_scale", (2, D)),
        ("pool_w_out", (2, D, D)),
        ("mla_w_in", (D, 1856)), ("mla_w_krp", (D, 64)), ("mla_q_norm", (512,)),
        ("mla_wq_pack", (8, 512, 256)), ("mla_kv_norm", (256,)), ("mla_w_ukv", (256, 2048)),
        ("mla_w_out", (D, D)),
        ("na_w_pack", (8, D, 512)), ("na_w_out", (D, D)), ("na_tab", (2, 16, 128, NTAB * 64)),
        ("rope_cos", (64, 2048)), ("rope_sin", (64, 2048)),
        ("rcnt_p", (4, 512)), ("rcnt_s", (4, 2048)), ("rcnt_w", (4, 1536)), ("ident", (128, 128)),
        ("rope_cos_w", (64, 1536)), ("rope_sin_w", (64, 1536)), ("wsel", (128, 2)),
    ]:
        I[name] = din(name, shape)
    O = {
        "yp": dout("yp", (512, D)), "ys": dout("ys", (1536, D)),
        "st_ckv": dout("st_ckv", (512, 256)), "st_kr": dout("st_kr", (512, 64)),
        "st_k": dout("st_k", (512, D)), "st_v": dout("st_v", (512, D)),
    }

    with ExitStack() as es:
        P = Prog(nc, es)

        def T(name, shape, dt):
            return es.enter_context(nc.sbuf_tensor("sb_" + name, list(shape), dt))

        ident = T("ident", [128, 128], F32)
        ones_bf = T("ones_bf", [128, 128], BF16)
        epsc = T("epsc", [128, 2], F32)
        ident_bf = T("ident_bf", [128, 128], BF16)
        lngT = T("lngT", [128, 4, 8], F32)
        lnbT = T("lnbT", [128, 4, 8], F32)
        adabT = T("adabT", [128, 4, 24], F32)
        pscT = T("pscT", [128, 2, 8], F32)
        qnT = T("qnT", [128, 4], F32)
        kvnT = T("kvnT", [128, 2], F32)
        condT = T("condT", [128, 2, 8], F32)
        wsel = T("wsel", [128, 2], F32)
        scondT = T("scondT", [128, 8, 2], BF16)
        mT = T("mT", [128, 4, 24, 2], F32)
        modv = T("modv", [128, 4, 2, 5, 8], F32)
        Yp = T("Yp", [128, 8, 2048], F32)
        H = T("H", [128, 8, 2048], BF16)
        ARENA_W = 27400
        arena = T("arena", [128, ARENA_W], F32)
        psb = [es.enter_context(nc.psum_tensor(f"ps{i}", [128, 512], F32)) for i in range(8)]
        psB = [Buf(f"ps{i}", excl=True) for i in range(8)]

        b_const = Buf("const")
        b_modv = Buf("modv")
        YpB = [[Buf() for _ in range(4)] for _ in range(8)]
        HB = [[Buf() for _ in range(4)] for _ in range(8)]

        class Arena:
            def __init__(self):
                self.off = 0

            def f32(self, n, shape=None):
                o = self.off
                self.off += n
                assert self.off <= ARENA_W, ("arena overflow", self.off)
                return arena[:, o:o + n]

            def bf16(self, n):
                w = (n + 1) // 2
                o = self.off
                self.off += w
                assert self.off <= ARENA_W, ("arena overflow", self.off)
                return arena[:, o:o + w].bitcast(BF16)[:, 0:n]

        def mm(out, lhsT, rhs, start, stop, reads, writes):
            P.op("pe", lambda e: e.matmul(out, lhsT, rhs, start=start, stop=stop), reads, writes)

        def tr(out, in_, idn, reads, writes):
            P.op("pe", lambda e: e.transpose(out, in_, idn), reads, writes)

        def act(out, in_, func, reads, writes, scale=None, bias=None):
            kw = {}
            if scale is not None:
                kw["scale"] = scale
            if bias is not None:
                kw["bias"] = bias
            P.op("act", lambda e: e.activation(out=out, in_=in_, func=func, **kw), reads, writes)

        def tt(out, in0, in1, op, reads, writes, q="dve"):
            P.op(q, lambda e: e.tensor_tensor(out=out, in0=in0, in1=in1, op=op), reads, writes)

        def ts(out, in0, s1, op0, reads, writes, s2=None, op1=None, q="dve"):
            if op1 is None:
                P.op(q, lambda e: e.tensor_scalar(out=out, in0=in0, scalar1=s1, scalar2=None, op0=op0), reads, writes)
            else:
                P.op(q, lambda e: e.tensor_scalar(out=out, in0=in0, scalar1=s1, scalar2=s2, op0=op0, op1=op1),
                     reads, writes)

        def stt(out, in0, scalar, in1, op0, op1, reads, writes):
            P.op("dve", lambda e: e.scalar_tensor_tensor(out=out, in0=in0, scalar=scalar, in1=in1, op0=op0, op1=op1),
                 reads, writes)

        def recip(out, in_, reads, writes):
            P.op("dve", lambda e: e.reciprocal(out=out, in_=in_), reads, writes)

        def recip_fast(out, in_, reads, writes):
            P.op("dve", lambda e: e.reciprocal_approx_fast(out=out, in_=in_), reads, writes)

        def memset(ap, v, writes, q="dve"):
            P.op(q, lambda e: e.memset(ap, v), (), writes)

        def load_cols(dst, src, n, width=128):
            bufs = []
            for i in range(n):
                b = Buf()
                P.dma("pool", dst[:, :, i * width:(i + 1) * width],
                      src[:, i * width:(i + 1) * width].rearrange("(c p) f -> p c f", p=128), writes=[b])
                bufs.append(b)
            return bufs

        mmrr = [0]
        mmpool = [[0, 1, 2, 3, 6, 7]]

        def mmbank():
            pool_ = mmpool[0]
            i = pool_[mmrr[0] % len(pool_)]
            mmrr[0] += 1
            return i

        P.dma("sp", ident[:], I["ident"], writes=[b_const])
        memset(ones_bf[:], 1.0, [b_const])
        act(ident_bf[:], ident[:], AF.Identity, [b_const], [b_const])
        memset(epsc[:, 0:1], LN_EPS, [b_const])
        memset(epsc[:, 1:2], RMS_EPS, [b_const])
        P.dma("sp", lngT[:], I["ln_g"].rearrange("l (c p) -> p l c", p=128), writes=[b_const])
        P.dma("sp", lnbT[:], I["ln_b"].rearrange("l (c p) -> p l c", p=128), writes=[b_const])
        P.dma("sp", adabT[:], I["ada_b"].rearrange("l (c p) -> p l c", p=128), writes=[b_const])
        P.dma("sp", pscT[:], I["pool_scale"].rearrange("l (c p) -> p l c", p=128), writes=[b_const])
        P.dma("sp", wsel[:], I["wsel"], writes=[b_const])
        P.dma("sp", qnT[:], I["mla_q_norm"].rearrange("(c p) -> p c", p=128), writes=[b_const])
        P.dma("sp", kvnT[:], I["mla_kv_norm"].rearrange("(c p) -> p c", p=128), writes=[b_const])
        for g in range(2):
            P.dma("sp", condT[:, g, :], I["cond"][g].rearrange("(c p) -> p c", p=128), writes=[b_const])
            act(scondT[:, :, g], condT[:, g, :], AF.Silu, [b_const], [b_const])

        A0 = Arena()
        adaw = [A0.bf16(8 * 512).rearrange("p (c f) -> p c f", c=8) for _ in range(2)]
        adawB = [Buf(), Buf()]
        pi = 0
        for l in range(4):
            bank = 4 + (l % 2)
            for pc in range(6):
                s = pi % 2
                pi += 1
                P.dma("pool", adaw[s], I["ada_w"][l, :, pc * 512:(pc + 1) * 512].rearrange("(c p) f -> p c f", p=128),
                      writes=[adawB[s]])
                for jj in range(4):
                    j = pc * 4 + jj
                    for k in range(8):
                        mm(psb[bank][:, 2 * j:2 * j + 2], adaw[s][:, k, jj * 128:(jj + 1) * 128], scondT[:, k, :],
                           k == 0, k == 7, [adawB[s], b_const], [psB[bank]])
            for g in range(2):
                tt(mT[:, l, :, g], psb[bank][:, 0:48].rearrange("p (j g) -> p j g", g=2)[:, :, g], adabT[:, l, :],
                   ALU.add, [psB[bank], b_const], [b_modv])
        for l in range(4):
            for g in range(2):
                sh = mT[:, l, 0:8, g]
                sc = mT[:, l, 8:16, g]
                gt = mT[:, l, 16:24, g]
                Av, Bv, Gv, AGv, ABv = (modv[:, l, g, i, :] for i in range(5))
                if l == 0:
                    ts(Av, sc, 1.0, ALU.add, [b_modv], [b_modv])
                    ts(Bv, sh, 1.0, ALU.mult, [b_modv], [b_modv])
                else:
                    stt(Av, sc, 1.0, lngT[:, l - 1, :], ALU.add, ALU.mult, [b_modv, b_const], [b_modv])
                    stt(Bv, sc, 1.0, lnbT[:, l - 1, :], ALU.add, ALU.mult, [b_modv, b_const], [b_modv])
                    tt(Bv, Bv, sh, ALU.add, [b_modv], [b_modv])
                ts(Gv, gt, 1.0, ALU.mult, [b_modv], [b_modv])
                if l < 3:
                    ts(AGv, lngT[:, l, :], ALPHA, ALU.mult, [b_const], [b_modv])
                    ts(ABv, lnbT[:, l, :], ALPHA, ALU.mult, [b_const], [b_modv])
                else:
                    ts(AGv, lngT[:, l, :], 1.0, ALU.mult, [b_const], [b_modv])
                    ts(ABv, lnbT[:, l, :], 1.0, ALU.mult, [b_const], [b_modv])

        def mv(l, g, i, c):
            return modv[:, l, g, i, c:c + 1]

        def load_x(G, xin):
            P.barrier()
            A = Arena()
            stg = [A.f32(4 * D).rearrange("p (t f) -> p t f", t=4) for _ in range(2)]
            stgB = [Buf(), Buf()]
            for blk in range(G.nblk):
                s = blk % 2
                P.dma("sp", stg[s], xin[blk * 512:(blk + 1) * 512, :].rearrange("(t p) f -> p t f", p=128),
                      writes=[stgB[s]])
                cs = slice(blk * 512, (blk + 1) * 512)
                for oc in range(8):
                    bk = mmbank()
                    for t4 in range(4):
                        tr(psb[bk][:, t4 * 128:(t4 + 1) * 128], stg[s][:, t4, oc * 128:(oc + 1) * 128], ident[:],
                           [stgB[s], b_const], [psB[bk]])
                    ts(Yp[:, oc, cs], psb[bk][:], ALPHA, ALU.mult, [psB[bk]], [YpB[oc][blk]])
                    act(H[:, oc, cs], psb[bk][:], AF.Identity, [psB[bk], b_modv], [HB[oc][blk]],
                        scale=mv(0, G.gi, 0, oc), bias=mv(0, G.gi, 1, oc))

        def outproj_epilogue(G, l, A, w_out, woB, Gsrc, GsrcB, blk, yout, inter=None):
            gi = G.gi
            cs = slice(blk * 512, (blk + 1) * 512)
            if "zb" not in A.__dict__:
                A.zb = [A.bf16(512) for _ in range(2)]
                A.zq = [A.bf16(512) for _ in range(2)]
                A.zbB = [Buf(), Buf()]
                A.zqB = [Buf(), Buf()]
                A.mean = A.f32(512)
                A.msq = A.f32(512)
                A.rstd = A.f32(512)
                A.nmr = A.f32(512)
                A.stB = Buf()
                if l == 3:
                    A.ostg = [A.f32(D) for _ in range(2)]
                    A.ostgB = [Buf(), Buf()]
            zb, zq, zbB, zqB = A.zb, A.zq, A.zbB, A.zqB
            mean, msq, rstd, nmr, stB = A.mean, A.msq, A.rstd, A.nmr, A.stB
            SUM, SQ = 4, 5
            pend = None
            for oc in range(8):
                bk = mmbank()
                for k in range(8):
                    mm(psb[bk][:], w_out[:, k, oc * 128:(oc + 1) * 128], Gsrc(k), k == 0, k == 7,
                       [woB[oc], GsrcB(k)], [psB[bk]])
                if pend is not None:
                    po, ps_ = pend
                    mm(psb[SUM][:], ones_bf[:], zb[ps_], po == 0, po == 7, [zbB[ps_], b_const], [psB[SUM]])
                    mm(psb[SQ][:], ones_bf[:], zq[ps_], po == 0, po == 7, [zqB[ps_], b_const], [psB[SQ]])
                stt(Yp[:, oc, cs], psb[bk][:], mv(l, gi, 2, oc), Yp[:, oc, cs], ALU.mult, ALU.add,
                    [psB[bk], YpB[oc][blk], b_modv], [YpB[oc][blk]])
                s = oc % 2
                act(zb[s], Yp[:, oc, cs], AF.Identity, [YpB[oc][blk]], [zbB[s]])
                act(zq[s], Yp[:, oc, cs], AF.Square, [YpB[oc][blk]], [zqB[s]])
                pend = (oc, s)
                if inter:
                    inter.pop(0)()
            po, ps_ = pend
            mm(psb[SUM][:], ones_bf[:], zb[ps_], False, True, [zbB[ps_], b_const], [psB[SUM]])
            mm(psb[SQ][:], ones_bf[:], zq[ps_], False, True, [zqB[ps_], b_const], [psB[SQ]])

            return epilogue_tail(G, l, A, blk, yout, SUM, SQ)

        def run_all(pieces):
            while pieces:
                pieces.pop(0)()

        def epilogue_tail(G, l, A, blk, yout, SUM, SQ):
            pieces = []
            pieces.append(lambda: tail_chain(G, l, A, blk, SUM, SQ))
            for oc in range(9):
                def piece(oc=oc):
                    if oc < 8:
                        tail_oc(G, l, A, blk, oc, "dve")
                    if oc >= 1:
                        tail_oc(G, l, A, blk, oc - 1, "act")
                pieces.append(piece)
            if l == 3:
                pieces.append(lambda: tail_out(G, l, A, blk, yout))
            return pieces

        def tail_chain(G, l, A, blk, SUM, SQ):
            mean, msq, rstd, nmr, stB = A.mean, A.msq, A.rstd, A.nmr, A.stB
            ts(mean, psb[SUM][:], 1.0 / D, ALU.mult, [psB[SUM]], [stB])
            tt(msq, mean, mean, ALU.mult, [stB], [stB])
            stt(msq, psb[SQ][:], 1.0 / D, msq, ALU.mult, ALU.subtract, [psB[SQ], stB], [stB])
            act(msq, msq, AF.Sqrt, [stB, b_const], [stB], bias=epsc[:, 0:1], scale=1.0)
            recip(rstd, msq, [stB], [stB])
            stt(nmr, mean, -1.0, rstd, ALU.mult, ALU.mult, [stB], [stB])

        def tail_oc(G, l, A, blk, oc, part):
            gi = G.gi
            cs = slice(blk * 512, (blk + 1) * 512)
            rstd, nmr, stB = A.rstd, A.nmr, A.stB
            if part == "dve":
                tt(Yp[:, oc, cs], Yp[:, oc, cs], rstd, ALU.mult, [YpB[oc][blk], stB], [YpB[oc][blk]])
                tt(Yp[:, oc, cs], Yp[:, oc, cs], nmr, ALU.add, [YpB[oc][blk], stB], [YpB[oc][blk]])
                return
            if l < 3:
                act(H[:, oc, cs], Yp[:, oc, cs], AF.Identity, [YpB[oc][blk], b_modv], [HB[oc][blk]],
                    scale=mv(l + 1, gi, 0, oc), bias=mv(l + 1, gi, 1, oc))
            act(Yp[:, oc, cs], Yp[:, oc, cs], AF.Identity, [YpB[oc][blk], b_modv], [YpB[oc][blk]],
                scale=mv(l, gi, 3, oc), bias=mv(l, gi, 4, oc))

        def tail_out(G, l, A, blk, yout):
            if True:
                ostg, ostgB = A.ostg, A.ostgB
                for t4 in range(4):
                    s = t4 % 2
                    for half in range(2):
                        bk = mmbank()
                        for o4 in range(4):
                            oc = half * 4 + o4
                            tr(psb[bk][:, o4 * 128:(o4 + 1) * 128],
                               Yp[:, oc, blk * 512 + t4 * 128: blk * 512 + (t4 + 1) * 128], ident[:],
                               [YpB[oc][blk], b_const], [psB[bk]])
                        P.op("act", lambda e, o=ostg[s][:, half * 512:(half + 1) * 512], i=psb[bk][:]:
                             e.activation(out=o, in_=i, func=AF.Identity), [psB[bk]], [ostgB[s]])
                    r0 = blk * 512 + t4 * 128
                    P.dma("sp", yout[r0:r0 + 128, :], ostg[s], reads=[ostgB[s]], is_out=True)

        def pool_layer(G, l, j, yout):
            gi = G.gi
            NT = G.NT
            P.barrier()
            mmpool[0] = [0, 1, 2, 3, 6, 7]
            A = Arena()
            mixed = A.bf16(8 * NT).rearrange("p (c n) -> p c n", c=8)
            mixB = [Buf() for _ in range(8)]
            mark = A.off
            pos = []
            p = 8
            for (s0, ln) in G.seqs:
                pos.append(p)
                p += ln + 8
            LB = p
            U = [A.f32(LB) for _ in range(2)]
            UB = [Buf(), Buf()]
            Sa = A.f32(LB)
            Sb = A.f32(LB)
            SB_ = Buf()
            rc = A.f32(NT)
            rcB = Buf()
            wu = [A.bf16(8 * 128).rearrange("p (c f) -> p c f", c=8) for _ in range(2)]
            wuB = [Buf(), Buf()]
            for s in range(2):
                memset(U[s], 0.0, [UB[s]])
            rsrc = I["rcnt_w"] if G.win else (I["rcnt_s"] if G.sample else I["rcnt_p"])
            for oc in range(8):
                s = oc % 2
                g = oc // 2
                P.dma("pool", wu[s], I["pool_w_in"][j, :, oc * 128:(oc + 1) * 128].rearrange("(c p) f -> p c f", p=128),
                      writes=[wuB[s]])
                if oc % 2 == 0:
                    P.dma("sp", rc, rsrc[g].partition_broadcast(128), writes=[rcB])
                for blk in range(G.nblk):
                    bk = mmbank()
                    for k in range(8):
                        mm(psb[bk][:], wu[s][:, k, :], H[:, k, blk * 512:(blk + 1) * 512], k == 0, k == 7,
                           [wuB[s], HB[k][blk]], [psB[bk]])
                    if G.sample:
                        P.op("act", lambda e, o=U[s][:, 8 + blk * 512: 8 + (blk + 1) * 512], i=psb[bk][:]:
                             e.activation(out=o, in_=i, func=AF.Identity), [psB[bk]], [UB[s]])
                    else:
                        for si in range(2):
                            P.op("act", lambda e, o=U[s][:, pos[si]:pos[si] + 256], i=psb[bk][:, si * 256:(si + 1) * 256]:
                                 e.activation(out=o, in_=i, func=AF.Identity), [psB[bk]], [UB[s]])
                u = U[s]
                tt(Sa[:, 1:LB], u[:, 0:LB - 1], u[:, 1:LB], ALU.add, [UB[s]], [SB_])
                cur = Sa
                oth = Sb
                if g >= 1:
                    tt(Sb[:, 2:LB - 1], Sa[:, 1:LB - 2], Sa[:, 3:LB], ALU.add, [SB_], [SB_])
                    cur, oth = Sb, Sa
                if g >= 2:
                    tt(Sa[:, 4:LB - 3], Sb[:, 2:LB - 5], Sb[:, 6:LB - 1], ALU.add, [SB_], [SB_])
                    cur, oth = Sa, Sb
                if g >= 3:
                    tt(Sb[:, 8:LB - 7], Sa[:, 4:LB - 11], Sa[:, 12:LB - 3], ALU.add, [SB_], [SB_])
                    cur, oth = Sb, Sa
                for si, (s0, ln) in enumerate(G.seqs):
                    p0 = pos[si]
                    tt(oth[:, p0:p0 + ln], cur[:, p0:p0 + ln], rc[:, s0:s0 + ln], ALU.mult, [SB_, rcB], [SB_])
                    tt(mixed[:, oc, s0:s0 + ln], oth[:, p0:p0 + ln], u[:, p0:p0 + ln], ALU.subtract,
                       [SB_, UB[s]], [mixB[oc]])
            if G.NT > 512:
                P.barrier()
                A.off = mark
            wz = A.bf16(8 * 1024).rearrange("p (c f) -> p c f", c=8)
            wo = A.bf16(8 * 1024).rearrange("p (c f) -> p c f", c=8)
            wg = A.bf16(4 * 2 * 256).rearrange("p (g k f) -> p g k f", g=4, k=2)
            wgB = Buf()
            P.dma("pool", wg, I["pool_w_grp"][j].rearrange("g (k p) f -> p g k f", p=128), writes=[wgB])
            wzB = load_cols(wz, I["pool_w_in"][j, :, 1024:2048], 8)
            woB = load_cols(wo, I["pool_w_out"][j], 8)
            Gt = A.bf16(8 * 512).rearrange("p (c n) -> p c n", c=8)
            GtB = [Buf() for _ in range(8)]
            st = [A.f32(512) for _ in range(2)]
            stB = [Buf(), Buf()]
            a_mark = A.off
            pend_tail = None
            mmpool[0] = [0, 1, 2, 3, 6, 7]
            for blk in range(G.nblk):
                cs = slice(blk * 512, (blk + 1) * 512)
                for oc in range(8):
                    g = oc // 2
                    jj = oc % 2
                    bz = mmbank()
                    for k in range(8):
                        mm(psb[bz][:], wz[:, k, oc * 128:(oc + 1) * 128], H[:, k, cs], k == 0, k == 7,
                           [wzB[oc], HB[k][blk]], [psB[bz]])
                    bm = mmbank()
                    for kk in range(2):
                        mm(psb[bm][:], wg[:, g, kk, jj * 128:(jj + 1) * 128], mixed[:, 2 * g + kk, cs], kk == 0, kk == 1,
                           [wgB, mixB[2 * g + kk]], [psB[bm]])
                    s = oc % 2
                    act(st[s], psb[bz][:], AF.Silu, [psB[bz]], [stB[s]])
                    stt(Gt[:, oc, :], psb[bm][:], pscT[:, j, oc:oc + 1], st[s], ALU.mult, ALU.mult,
                        [psB[bm], stB[s], b_const], [GtB[oc]])
                    if pend_tail:
                        pend_tail.pop(0)()
                if pend_tail:
                    run_all(pend_tail)
                pend_tail = outproj_epilogue(G, l, A, wo, woB, lambda k: Gt[:, k, :], lambda k: GtB[k], blk, yout)
            run_all(pend_tail)

        def mla_layer(G, l, yout):
            gi = G.gi
            NT, NK, NKL = G.NT, G.NK, G.NKL
            nkc = NK // 128
            P.barrier()
            mmpool[0] = [0, 1, 2, 3, 6, 7]
            A = Arena()
            Zs = A.bf16(8 * NT).rearrange("p (c n) -> p c n", c=8)
            ZsB = [[Buf() for _ in range(G.nblk)] for _ in range(8)]
            cqn = A.bf16(4 * NT).rearrange("p (c n) -> p c n", c=4)
            cqnB = [Buf() for _ in range(G.nblk)]
            ckvn = A.bf16(2 * NK).rearrange("p (c n) -> p c n", c=2)
            ckvnB = Buf()
            krT = A.bf16(NK)
            krTB = Buf()
            if G.sample:
                cosT = A.f32(NKL)
                sinT = A.f32(NKL)
                ropeB = Buf()
                P.dma("sp", cosT[0:64, :], I["rope_cos"], writes=[ropeB])
                P.dma("sp", sinT[0:64, :], I["rope_sin"], writes=[ropeB])
            if G.sample:
                t1 = A.f32(512)
                t2 = A.f32(512)
                tB = Buf()
            zs_mark = 4 * NT
            mark = A.off
            wa = A.bf16(8 * 832).rearrange("p (c f) -> p c f", c=8)
            waKV = Buf()
            waQ = Buf()
            P.dma("pool", wa[:, :, 512:832], I["mla_w_in"][:, 512:832].rearrange("(c p) f -> p c f", p=128), writes=[waKV])
            if G.sample:
                wkp = A.bf16(8 * 64).rearrange("p (c f) -> p c f", c=8)
                P.dma("pool", wkp, I["mla_w_krp"].rearrange("(c p) f -> p c f", p=128), writes=[waKV])
            P.dma("pool", wa[:, :, 0:512], I["mla_w_in"][:, 0:512].rearrange("(c p) f -> p c f", p=128), writes=[waQ])
            wzp = [A.bf16(8 * 128).rearrange("p (c f) -> p c f", c=8) for _ in range(2)]
            wzpB = [Buf(), Buf()]
            sq = [A.bf16(512) for _ in range(2)]
            sqB = [Buf(), Buf()]
            rsl = [A.f32(512) for _ in range(2)]
            rslB = [Buf(), Buf()]
            x32 = [A.f32(4 * 512).rearrange("p (c n) -> p c n", c=4)] * 2
            x32B = [Buf()] * 2
            rmsc = [0]
            if not G.sample:
                ck32 = A.f32(2 * 512).rearrange("p (c n) -> p c n", c=2)
                kr32 = A.f32(512)
                o32B = Buf()
                sto = [A.f32(256) for _ in range(4)]
                stoB = [Buf() for _ in range(4)]
                stk = [A.f32(64) for _ in range(4)]
                stkB = [Buf() for _ in range(4)]
            else:
                cstg = A.f32(2 * 256).rearrange("p (t f) -> p t f", t=2)
                kstg = A.f32(2 * 64).rearrange("p (t f) -> p t f", t=2)
                cstgB = Buf()

            def rms_group(banks, nfeat, normT, dst, dstB, blk, extra32=None):
                cs = slice(blk * 512, (blk + 1) * 512)
                nb = len(banks)
                ST = 4
                xs_ = x32[rmsc[0] % 2]
                xsB = x32B[rmsc[0] % 2]
                rs_ = rsl[rmsc[0] % 2]
                rsB_ = rslB[rmsc[0] % 2]
                rmsc[0] += 1
                for c, bk in enumerate(banks):
                    s = c % 2
                    act(sq[s], psb[bk][:], AF.Square, [psB[bk]], [sqB[s]])
                    act(xs_[:, c, :], psb[bk][:], AF.Identity, [psB[bk]], [xsB])
                    mm(psb[ST][:], ones_bf[:], sq[s], c == 0, c == nb - 1, [sqB[s], b_const], [psB[ST]])
                ts(rs_, psb[ST][:], 1.0 / nfeat, ALU.mult, [psB[ST]], [rsB_])
                act(rs_, rs_, AF.Sqrt, [rsB_, b_const], [rsB_], bias=epsc[:, 1:2], scale=1.0)
                recip(rs_, rs_, [rsB_], [rsB_])
                for c, bk in enumerate(banks):
                    if extra32 is not None:
                        stt(extra32[:, c, :], xs_[:, c, :], normT[:, c:c + 1], rs_, ALU.mult, ALU.mult,
                            [xsB, rsB_, b_const], [o32B])
                    stt(dst[:, c, cs], xs_[:, c, :], normT[:, c:c + 1], rs_, ALU.mult, ALU.mult,
                        [xsB, rsB_, b_const], [dstB])

            for blk in range(G.nkblk):
                cs = slice(blk * 512, (blk + 1) * 512)
                bks = [mmbank(), mmbank()]
                for c in range(2):
                    for k in range(8):
                        mm(psb[bks[c]][:], wa[:, k, 512 + c * 128:512 + (c + 1) * 128], H[:, k, cs], k == 0, k == 7,
                           [waKV, HB[k][blk]], [psB[bks[c]]])
                rms_group(bks, 256, kvnT, ckvn, ckvnB, blk, extra32=None if G.sample else ck32)
                b2 = mmbank()
                for k in range(8):
                    mm(psb[b2][0:64, :], wa[:, k, 768:832], H[:, k, cs], k == 0, k == 7, [waKV, HB[k][blk]], [psB[b2]])
                if G.sample:
                    b3 = mmbank()
                    for k in range(8):
                        mm(psb[b3][0:64, :], wkp[:, k, :], H[:, k, cs], k == 0, k == 7, [waKV, HB[k][blk]], [psB[b3]])
                    tt(t1[0:64, :], psb[b2][0:64, :], cosT[0:64, cs], ALU.mult, [psB[b2], ropeB], [tB])
                    tt(t2[0:64, :], psb[b3][0:64, :], sinT[0:64, cs], ALU.mult, [psB[b3], ropeB], [tB])
                    tt(krT[0:64, cs], t1[0:64, :], t2[0:64, :], ALU.add, [tB], [krTB])
                else:
                    act(krT[0:64, cs], psb[b2][0:64, :], AF.Identity, [psB[b2]], [krTB])
                    ts(kr32[0:64, :], psb[b2][0:64, :], 1.0, ALU.mult, [psB[b2]], [o32B])
                    for t4 in range(4):
                        s = t4
                        bk = mmbank()
                        for c in range(2):
                            tr(psb[bk][:, c * 128:(c + 1) * 128], ck32[:, c, t4 * 128:(t4 + 1) * 128], ident[:],
                               [o32B, b_const], [psB[bk]])
                        tr(psb[bk][:, 256:320], kr32[0:64, t4 * 128:(t4 + 1) * 128], ident[0:64, 0:64],
                           [o32B, b_const], [psB[bk]])
                        P.op("act", lambda e, o=sto[s], i=psb[bk][:, 0:256]: e.activation(out=o, in_=i, func=AF.Identity),
                             [psB[bk]], [stoB[s]])
                        P.op("act", lambda e, o=stk[s], i=psb[bk][:, 256:320]: e.activation(out=o, in_=i, func=AF.Identity),
                             [psB[bk]], [stkB[s]])
                        r0 = blk * 512 + t4 * 128
                        P.dma("sp", O["st_ckv"][r0:r0 + 128, :], sto[s], reads=[stoB[s]], is_out=True)
                        P.dma("sp", O["st_kr"][r0:r0 + 128, :], stk[s], reads=[stkB[s]], is_out=True)
            if G.sample:
                P.dma("sp", cstg, I["c_ckv"].rearrange("(t p) f -> p t f", p=128), writes=[cstgB])
                P.dma("sp", kstg, I["c_kr"].rearrange("(t p) f -> p t f", p=128), writes=[cstgB])
                bk = mmbank()
                for c in range(2):
                    for t2_ in range(2):
                        tr(psb[bk][:, (c * 2 + t2_) * 128:(c * 2 + t2_ + 1) * 128], cstg[:, t2_, c * 128:(c + 1) * 128],
                           ident[:], [cstgB, b_const], [psB[bk]])
                for c in range(2):
                    act(ckvn[:, c, NKL:NKL + 256], psb[bk][:, c * 256:(c + 1) * 256], AF.Identity, [psB[bk]], [ckvnB])
                bk = mmbank()
                for t2_ in range(2):
                    tr(psb[bk][0:64, t2_ * 128:(t2_ + 1) * 128], kstg[:, t2_, :], ident[:], [cstgB, b_const], [psB[bk]])
                act(krT[0:64, NKL:NKL + 256], psb[bk][0:64, 0:256], AF.Identity, [psB[bk]], [krTB])
            if G.win:
                def blend(X, XB, blk):
                    ca = slice(blk * 512, (blk + 1) * 512)
                    cb = slice((blk + 1) * 512, (blk + 2) * 512)
                    for oc in range(8):
                        ts(X[:, oc, ca], X[:, oc, ca], wsel[:, 0:1], ALU.mult, [XB[oc][blk], b_const], [XB[oc][blk]])
                        stt(X[:, oc, ca], X[:, oc, cb], wsel[:, 1:2], X[:, oc, ca], ALU.mult, ALU.add,
                            [XB[oc][blk + 1], XB[oc][blk], b_const], [XB[oc][blk]])

                for blk in range(3):
                    blend(H, HB, blk)
                P.dma("sp", cosT[0:64, 0:NT], I["rope_cos_w"], reads=[krTB], writes=[ropeB])
                P.dma("sp", sinT[0:64, 0:NT], I["rope_sin_w"], reads=[krTB], writes=[ropeB])
            for blk in range(G.nblk):
                cs = slice(blk * 512, (blk + 1) * 512)
                banks = [mmbank() for _ in range(4)]
                for c in range(4):
                    for k in range(8):
                        mm(psb[banks[c]][:], wa[:, k, c * 128:(c + 1) * 128], H[:, k, cs], k == 0, k == 7,
                           [waQ, HB[k][blk]], [psB[banks[c]]])
                rms_group(banks, 512, qnT, cqn, cqnB[blk], blk)
            if G.win:
                for blk in range(3):
                    blend(Yp, YpB, blk)
            for oc in range(8):
                s = oc % 2
                P.dma("pool", wzp[s], I["mla_w_in"][:, 832 + oc * 128:832 + (oc + 1) * 128].rearrange("(c p) f -> p c f", p=128),
                      writes=[wzpB[s]])
                for blk in range(G.nblk):
                    cs = slice(blk * 512, (blk + 1) * 512)
                    bk = mmbank()
                    for k in range(8):
                        mm(psb[bk][:], wzp[s][:, k, :], H[:, k, cs], k == 0, k == 7, [wzpB[s], HB[k][blk]], [psB[bk]])
                    act(Zs[:, oc, cs], psb[bk][:], AF.Silu, [psB[bk]], [ZsB[oc][blk]])

            small = G.NT <= 512
            if not small:
                P.barrier()
                A.off = mark
            mmpool[0] = [0, 1, 2, 3]
            wq_l = [A.bf16(4 * 256).rearrange("p (c f) -> p c f", c=4) for _ in range(2)]
            wkv_l = [A.bf16(2 * 256).rearrange("p (c f) -> p c f", c=2) for _ in range(2)]
            whB_l = [Buf(), Buf()]
            Hf = H[:].rearrange("p c n -> p (c n)")
            ho = [0]

            def hb(n):
                if small:
                    return A.bf16(n)
                o = ho[0]
                ho[0] += n
                assert ho[0] <= 8 * 2048
                return Hf[:, o:o + n]

            qn_l = [hb(NT) for _ in range(2)]
            qr_l = [hb(NT) for _ in range(2)]
            kn_l = [hb(NK) for _ in range(2)]
            Vh_l = [hb(nkc * 128).rearrange("p (k f) -> p k f", k=nkc) for _ in range(2)]
            hB_l = [[Buf() for _ in range(4)] for _ in range(2)]
            pt = [A.bf16(512) for _ in range(3)]
            ptB = [Buf() for _ in range(3)]
            rD = A.f32(512)
            tO = A.f32(512)
            rB = Buf()
            if G.sample:
                qblocks = [(b * 512, 512, list(range(nkc)), b) for b in range(G.nblk)]
            else:
                qblocks = [(s * 256, 256, [2 * s, 2 * s + 1], 0) for s in range(2)]
            ptc = 0
            accset = 0
            def emit_proj(h):
                s2 = h % 2
                wq, wkv, whB = wq_l[s2], wkv_l[s2], whB_l[s2]
                qn, qr, kn, Vh = qn_l[s2], qr_l[s2], kn_l[s2], Vh_l[s2]
                qnB, qrB, knB, VhB = hB_l[s2]
                P.dma("pool", wq, I["mla_wq_pack"][h].rearrange("(c p) f -> p c f", p=128), writes=[whB])
                P.dma("pool", wkv, I["mla_w_ukv"][:, h * 256:(h + 1) * 256].rearrange("(c p) f -> p c f", p=128), writes=[whB])
                for blk in range(G.nblk):
                    cs = slice(blk * 512, (blk + 1) * 512)
                    bk = mmbank()
                    for c in range(4):
                        mm(psb[bk][:], wq[:, c, 0:128], cqn[:, c, cs], c == 0, c == 3, [whB, cqnB[blk]], [psB[bk]])
                    act(qn[:, cs], psb[bk][:], AF.Identity, [psB[bk]], [qnB], scale=MLA_SCALE)
                    bk = mmbank()
                    for c in range(4):
                        mm(psb[bk][0:64, :], wq[:, c, 128:192], cqn[:, c, cs], c == 0, c == 3, [whB, cqnB[blk]], [psB[bk]])
                    if G.sample:
                        bk2 = mmbank()
                        for c in range(4):
                            mm(psb[bk2][0:64, :], wq[:, c, 192:256], cqn[:, c, cs], c == 0, c == 3, [whB, cqnB[blk]], [psB[bk2]])
                        stt(t1[0:64, :], psb[bk][0:64, :], MLA_SCALE, cosT[0:64, cs], ALU.mult, ALU.mult, [psB[bk], ropeB], [tB])
                        stt(t2[0:64, :], psb[bk2][0:64, :], MLA_SCALE, sinT[0:64, cs], ALU.mult, ALU.mult, [psB[bk2], ropeB], [tB])
                        tt(qr[0:64, cs], t1[0:64, :], t2[0:64, :], ALU.add, [tB], [qrB])
                    else:
                        act(qr[0:64, cs], psb[bk][0:64, :], AF.Identity, [psB[bk]], [qrB], scale=MLA_SCALE)
                for kb in range(0, NK, 512):
                    w = min(512, NK - kb)
                    bk = mmbank()
                    for c in range(2):
                        mm(psb[bk][:, 0:w], wkv[:, c, 0:128], ckvn[:, c, kb:kb + w], c == 0, c == 1, [whB, ckvnB], [psB[bk]])
                    act(kn[:, kb:kb + w], psb[bk][:, 0:w], AF.Identity, [psB[bk]], [knB])
                for k0 in range(0, nkc, 4):
                    n4 = min(4, nkc - k0)
                    bk = mmbank()
                    for i4 in range(n4):
                        kc = k0 + i4
                        for c in range(2):
                            mm(psb[bk][:, i4 * 128:(i4 + 1) * 128], ckvn[:, c, kc * 128:(kc + 1) * 128], wkv[:, c, 128:256],
                               c == 0, c == 1, [whB, ckvnB], [psB[bk]])
                    P.op("dve", lambda e, o=Vh[:, k0:k0 + n4, :].rearrange("p k f -> p (k f)"), i=psb[bk][:, 0:n4 * 128]:
                         e.tensor_copy(out=o, in_=i), [psB[bk]], [VhB])
            def emit_attn(h, inject):
                nonlocal ptc, accset
                s2 = h % 2
                qn, qr, kn, Vh = qn_l[s2], qr_l[s2], kn_l[s2], Vh_l[s2]
                qnB, qrB, knB, VhB = hB_l[s2]
                for qi, (q0, nq, kcs, zblk) in enumerate(qblocks):
                    if qi == 1 and inject is not None:
                        inject()
                    qs = slice(q0, q0 + nq)
                    OB, DB = (4, 5) if accset % 2 == 0 else (6, 7)
                    accset += 1
                    sbanks = {}

                    def qk_a(kc):
                        bk = mmbank()
                        sbanks[kc] = bk
                        mm(psb[bk][:, 0:nq], kn[:, kc * 128:(kc + 1) * 128], qn[:, qs], True, False, [knB, qnB], [psB[bk]])

                    def qk_b(kc):
                        bk = sbanks[kc]
                        mm(psb[bk][:, 0:nq], krT[0:64, kc * 128:(kc + 1) * 128], qr[0:64, qs], False, True,
                           [krTB, qrB], [psB[bk]])

                    qk_a(kcs[0])
                    qk_b(kcs[0])
                    for ii, kc in enumerate(kcs):
                        nxt = kcs[ii + 1] if ii + 1 < len(kcs) else None
                        if nxt is not None:
                            qk_a(nxt)
                            qk_b(nxt)
                        bk = sbanks[kc]
                        pi_ = ptc % 3
                        ptc += 1
                        act(pt[pi_][:, 0:nq], psb[bk][:, 0:nq], AF.Exp, [psB[bk]], [ptB[pi_]])
                        mm(psb[OB][:, 0:nq], Vh[:, kc, :], pt[pi_][:, 0:nq], ii == 0, ii == len(kcs) - 1,
                           [VhB, ptB[pi_]], [psB[OB]])
                        mm(psb[DB][:, 0:nq], ones_bf[:], pt[pi_][:, 0:nq], ii == 0, ii == len(kcs) - 1,
                           [b_const, ptB[pi_]], [psB[DB]])
                    recip(rD[:, 0:nq], psb[DB][:, 0:nq], [psB[DB]], [rB])
                    tt(tO[:, 0:nq], psb[OB][:, 0:nq], rD[:, 0:nq], ALU.mult, [psB[OB], rB], [rB])
                    tt(Zs[:, h, qs], tO[:, 0:nq], Zs[:, h, qs], ALU.mult, [rB, ZsB[h][zblk]], [ZsB[h][zblk]])

            emit_proj(0)
            for h in range(8):
                emit_attn(h, (lambda h=h: emit_proj(h + 1)) if h < 7 else None)
            if not small:
                P.barrier()
                A.off = zs_mark
            mmpool[0] = [0, 1, 2, 3, 6, 7]
            wo = A.bf16(8 * 1024).rearrange("p (c f) -> p c f", c=8)
            woB = load_cols(wo, I["mla_w_out"], 8)
            pend_tail = None
            for blk in range(G.nblk):
                t_ = outproj_epilogue(G, l, A, wo, woB, lambda k, blk=blk: Zs[:, k, blk * 512:(blk + 1) * 512],
                                      lambda k, blk=blk: ZsB[k][blk], blk, yout, inter=pend_tail)
                if pend_tail:
                    run_all(pend_tail)
                pend_tail = t_
            run_all(pend_tail)

        def na_layer(G, l, yout):
            gi = G.gi
            NT, NK = G.NT, G.NK
            nkc = NK // 128
            P.barrier()
            mmpool[0] = [0, 1, 2, 3]
            A = Arena()
            Zs = A.bf16(8 * NT).rearrange("p (c n) -> p c n", c=8)
            ZsB = [[Buf() for _ in range(G.nblk)] for _ in range(8)]
            wp = [A.bf16(8 * 512).rearrange("p (c f) -> p c f", c=8) for _ in range(2)]
            wpB = [Buf(), Buf()]
            nkc = NT // 128 + 2
            NKA = nkc * 128
            qh_l = [A.bf16(NT) for _ in range(2)]
            kh_l = [A.bf16(NKA) for _ in range(2)]
            Vh_l = [A.bf16(nkc * 128).rearrange("p (k f) -> p k f", k=nkc) for _ in range(2)]
            qkvB_l = [[Buf(), Buf(), Buf()] for _ in range(2)]
            pt = [A.bf16(256) for _ in range(6)]
            ptB = [Buf() for _ in range(6)]
            rD = A.f32(256)
            tO = A.f32(256)
            rB = Buf()
            if G.sample:
                tabs_l = [[[A.bf16(NTAB * 64) for _ in range(2)] for _ in range(2)] for _ in range(2)]
                tabB_l = [[Buf(), Buf()] for _ in range(2)]
                kstg = A.f32(2 * 128).rearrange("p (t f) -> p t f", t=2)
                kstgB = Buf()
            else:
                k32 = A.f32(512)
                k32B = Buf()
                kst = [A.f32(128) for _ in range(4)]
                kstB = [Buf() for _ in range(4)]
                vst = [A.f32(128) for _ in range(4)]
                vstB = [Buf() for _ in range(4)]
            ptc = 0
            sbc = 0
            accset = 0
            def emit_proj(hg, part):
                s = hg % 2
                qh, kh, Vh = qh_l[s], kh_l[s], Vh_l[s]
                qB, kB, VB = qkvB_l[s]
                if G.sample:
                    tabs, tabB = tabs_l[s], tabB_l[s]
                w = wp[s]
                if part < 0:
                    P.dma("pool", wp[s], I["na_w_pack"][hg].rearrange("(c p) f -> p c f", p=128), writes=[wpB[s]])
                    if G.sample:
                        for hp in range(2):
                            for kind in range(2):
                                P.dma("pool", tabs[hp][kind], I["na_tab"][kind, 2 * hg + hp], writes=[tabB[hp]])
                        P.dma("sp", kstg, I["c_nak"][:, hg * 128:(hg + 1) * 128].rearrange("(t p) f -> p t f", p=128),
                              writes=[kstgB])
                        P.dma("pool", Vh[:, NT // 128:NT // 128 + 2, :], I["c_nav"][:, hg * 128:(hg + 1) * 128].rearrange("(t p) f -> p t f", p=128),
                              writes=[VB])
                        bk = mmbank()
                        for t2_ in range(2):
                            tr(psb[bk][:, t2_ * 128:(t2_ + 1) * 128], kstg[:, t2_, :], ident[:], [kstgB, b_const], [psB[bk]])
                        act(kh[:, NT:NT + 256], psb[bk][:, 0:256], AF.Identity, [psB[bk]], [kB])
                    return
                blk = part
                cs = slice(blk * 512, (blk + 1) * 512)
                bk = mmbank()
                for k in range(8):
                    mm(psb[bk][:], w[:, k, 0:128], H[:, k, cs], k == 0, k == 7, [wpB[s], HB[k][blk]], [psB[bk]])
                act(qh[:, cs], psb[bk][:], AF.Identity, [psB[bk]], [qB], scale=NA_SCALE)
                bk = mmbank()
                for k in range(8):
                    mm(psb[bk][:], w[:, k, 128:256], H[:, k, cs], k == 0, k == 7, [wpB[s], HB[k][blk]], [psB[bk]])
                act(kh[:, cs], psb[bk][:], AF.Identity, [psB[bk]], [kB])
                if not G.sample:
                    ts(k32, psb[bk][:], 1.0, ALU.mult, [psB[bk]], [k32B])
                    for t4 in range(4):
                        s2 = t4
                        bk2 = mmbank()
                        tr(psb[bk2][:, 0:128], k32[:, t4 * 128:(t4 + 1) * 128], ident[:], [k32B, b_const], [psB[bk2]])
                        P.op("act", lambda e, o=kst[s2], i=psb[bk2][:, 0:128]: e.activation(out=o, in_=i, func=AF.Identity),
                             [psB[bk2]], [kstB[s2]])
                        r0 = blk * 512 + t4 * 128
                        P.dma("sp", O["st_k"][r0:r0 + 128, hg * 128:(hg + 1) * 128], kst[s2], reads=[kstB[s2]], is_out=True)
                bk = mmbank()
                for t4 in range(4):
                    for k in range(8):
                        mm(psb[bk][:, t4 * 128:(t4 + 1) * 128], H[:, k, blk * 512 + t4 * 128: blk * 512 + (t4 + 1) * 128],
                           w[:, k, 256:384], k == 0, k == 7, [wpB[s], HB[k][blk]], [psB[bk]])
                P.op("dve", lambda e, o=Vh[:, blk * 4:(blk + 1) * 4, :].rearrange("p k f -> p (k f)"), i=psb[bk][:]:
                     e.tensor_copy(out=o, in_=i), [psB[bk]], [VB])
                if not G.sample:
                    for t4 in range(4):
                        s2 = t4
                        P.op("act", lambda e, o=vst[s2], i=psb[bk][:, t4 * 128:(t4 + 1) * 128]:
                             e.activation(out=o, in_=i, func=AF.Identity), [psB[bk]], [vstB[s2]])
                        r0 = blk * 512 + t4 * 128
                        P.dma("sp", O["st_v"][r0:r0 + 128, hg * 128:(hg + 1) * 128], vst[s2], reads=[vstB[s2]], is_out=True)
                bk = mmbank()
                for k in range(8):
                    mm(psb[bk][:], w[:, k, 384:512], H[:, k, cs], k == 0, k == 7, [wpB[s], HB[k][blk]], [psB[bk]])
                act(Zs[:, hg, cs], psb[bk][:], AF.Silu, [psB[bk]], [ZsB[hg][blk]])
            def emit_attn(hg, inject):
                nonlocal ptc
                s = hg % 2
                qh, kh, Vh = qh_l[s], kh_l[s], Vh_l[s]
                qB, kB, VB = qkvB_l[s]
                if G.sample:
                    tabs, tabB = tabs_l[s], tabB_l[s]
                items = []
                if G.sample:
                    ng = NT // 256
                    nkl = NT // 128
                    for g8 in range(ng):
                        if g8 == 0:
                            loc = [(kc, 1, 6 - 2 * kc) for kc in range(4)]
                        elif g8 == ng - 1:
                            loc = [(kc, 1, 6 - (2 * kc - 4 * g8)) for kc in range(2 * g8 - 2, 2 * g8 + 2)]
                        else:
                            loc = [(2 * g8 - 2 + jx, 0, 6 - (2 * jx - 4)) for jx in range(6)]
                        items.append((g8 * 256, 256, loc + [(nkl, None, None), (nkl + 1, None, None)], g8 // 2))
                else:
                    for s_ in range(2):
                        items.append((s_ * 256, 256, [(2 * s_, None, None), (2 * s_ + 1, None, None)], 0))
                steps = []
                for it_i, itm in enumerate(items):
                    kcl = itm[2]
                    for hp in range(2):
                        for b0 in range(0, len(kcl), 2):
                            steps.append((it_i, hp, kcl[b0:b0 + 2], b0))
                LA = 3

                def emit_qk(t):
                    it_i, hp, ents, b0 = steps[t]
                    q0, nq, kcl, zblk = items[it_i]
                    pr = slice(hp * 64, hp * 64 + 64)
                    bk = t % 4
                    for e_i, ent in enumerate(ents):
                        kc, kind, bb0 = ent
                        dst = psb[bk][:, e_i * 256:e_i * 256 + nq]
                        mm(dst, kh[pr, kc * 128:(kc + 1) * 128], qh[pr, q0:q0 + nq], e_i == 0, kind is None, [kB, qB], [psB[bk]])
                    for e_i, ent in enumerate(ents):
                        kc, kind, bb0 = ent
                        dst = psb[bk][:, e_i * 256:e_i * 256 + nq]
                        if kind is not None:
                            mm(dst, ident_bf[:], tabs[hp][kind][:, bb0 * 64: bb0 * 64 + 256], False, True,
                               [b_const, tabB[hp]], [psB[bk]])

                def emit_rest(t):
                    nonlocal ptc
                    it_i, hp, ents, b0 = steps[t]
                    q0, nq, kcl, zblk = items[it_i]
                    pr = slice(hp * 64, hp * 64 + 64)
                    qs = slice(q0, q0 + nq)
                    bk = t % 4
                    OB, DB = (4, 5) if it_i % 2 == 0 else (6, 7)
                    pts = []
                    for e_i, ent in enumerate(ents):
                        pi_ = ptc % 6
                        ptc += 1
                        act(pt[pi_][:, 0:nq], psb[bk][:, e_i * 256:e_i * 256 + nq], AF.Exp, [psB[bk]], [ptB[pi_]])
                        pts.append(pi_)
                    for e_i, ent in enumerate(ents):
                        kc = ent[0]
                        ci = b0 + e_i
                        pi_ = pts[e_i]
                        mm(psb[OB][pr, 0:nq], Vh[:, kc, pr], pt[pi_][:, 0:nq], ci == 0, ci == len(kcl) - 1,
                           [VB, ptB[pi_]], [psB[OB]])
                        mm(psb[DB][pr, 0:nq], ones_bf[:, pr], pt[pi_][:, 0:nq], ci == 0, ci == len(kcl) - 1,
                           [b_const, ptB[pi_]], [psB[DB]])
                    if hp == 1 and b0 + 2 >= len(kcl):
                        recip(rD[:, 0:nq], psb[DB][:, 0:nq], [psB[DB]], [rB])
                        tt(tO[:, 0:nq], psb[OB][:, 0:nq], rD[:, 0:nq], ALU.mult, [psB[OB], rB], [rB])
                        tt(Zs[:, hg, qs], tO[:, 0:nq], Zs[:, hg, qs], ALU.mult, [rB, ZsB[hg][zblk]], [ZsB[hg][zblk]])

                inj = list(inject) if inject else []
                npc = len(inj)
                nr = 0
                for t in range(len(steps)):
                    if inj and t >= (npc - len(inj) + 1) * len(steps) // (npc + 1):
                        while nr < t:
                            emit_rest(nr)
                            nr += 1
                        inj.pop(0)()
                    emit_qk(t)
                    while nr <= t - LA:
                        emit_rest(nr)
                        nr += 1
                while nr < len(steps):
                    emit_rest(nr)
                    nr += 1
                while inj:
                    inj.pop(0)()

            for hg in range(8):
                for part in range(-1, G.nblk):
                    emit_proj(hg, part)
                emit_attn(hg, None)
            if G.NT > 512:
                P.barrier()
                A.off = 4 * NT
            mmpool[0] = [0, 1, 2, 3, 6, 7]
            wo = A.bf16(8 * 1024).rearrange("p (c f) -> p c f", c=8)
            woB = load_cols(wo, I["na_w_out"], 8)
            pend_tail = None
            for blk in range(G.nblk):
                t_ = outproj_epilogue(G, l, A, wo, woB, lambda k, blk=blk: Zs[:, k, blk * 512:(blk + 1) * 512],
                                      lambda k, blk=blk: ZsB[k][blk], blk, yout, inter=pend_tail)
                if pend_tail:
                    run_all(pend_tail)
                pend_tail = t_
            run_all(pend_tail)

        GP = Grp(0, 512, [(0, 256), (256, 256)], False)
        GS = Grp(1, 2048, [(0, 2048)], True)
        GW = Grp(1, 1536, [(0, 1536)], True, NKL=2048, win=True)
        stage = [0]

        def go():
            stage[0] += 1
            return stage[0] <= stop

        for G, G2, xin, yout in ((GP, GP, I["xp"], O["yp"]), (GS, GW, I["xs"], O["ys"])):
            if go():
                load_x(G, xin)
            if go():
                pool_layer(G, 0, 0, yout)
            if go():
                mla_layer(G2, 1, yout)
            if go():
                na_layer(G2, 2, yout)
            if go():
                pool_layer(G2, 3, 1, yout)
        if dbg:
            P.barrier()
            dbg_out = nc.dram_tensor("dbg", [128, 8 * 512 + 4 * 2 * 5 * 8], F32, kind="ExternalOutput").ap()
            P.dma("sp", dbg_out[:, 0:4096].rearrange("p (c n) -> p c n", c=8), Yp[:, :, 0:512], is_out=True)
            P.dma("sp", dbg_out[:, 4096:4096 + 320], modv[:].rearrange("p l g i c -> p (l g i c)"), is_out=True)
        P.emit()
        print("pe marks", P.marks)
        print("instr counts", {q: len(P.lists[q]) for q in P.QS}, "max sem", {q: max([i.val or 0 for i in P.lists[q]] + [0]) for q in P.QS})
    return nc


def _rope_tables():
    nf = 16
    inv = (10000.0 ** (-np.arange(nf, dtype=np.float32) / nf)).astype(np.float32)
    t = np.arange(2048)
    pos = np.stack([t // GRID_W, t % GRID_W], -1).astype(np.float32)
    ang = (pos[:, :, None] * inv).astype(np.float32)
    cos = np.cos(ang).astype(np.float32)
    sin = np.sin(ang).astype(np.float32)
    C = np.zeros((64, 2048), np.float32)
    S = np.zeros((64, 2048), np.float32)
    for a in range(2):
        for s in range(2):
            for i in range(nf):
                f = a * 32 + s * 16 + i
                C[f] = cos[:, a, i]
                S[f] = sin[:, a, i] * (-1.0 if s == 0 else 1.0)
    perm = np.array([a * 32 + (1 - s) * 16 + i for a in range(2) for s in range(2) for i in range(nf)])
    return C, S, perm


def _na_table_index():
    idx = np.full((2, 128, NTAB, 64), 15 * 31, np.int64)
    qc = np.arange(64)
    cstart = np.clip(qc - 8, 0, 48)
    for kind in range(2):
        for krl in range(2):
            for kc in range(64):
                p = krl * 64 + kc
                colv = (kc >= cstart) & (kc < cstart + 16)
                for bb in range(NTAB):
                    dr = 6 - bb + krl
                    if dr < -7 or dr > 7:
                        continue
                    if kind == 0 and (dr < -4 or dr > 3):
                        continue
                    dc = kc - qc
                    v = (dr + 7) * 31 + (dc + 15)
                    idx[kind, p, bb, :] = np.where(colv, v, 15 * 31)
    return idx


def _rcnt(L, reps):
    t = np.arange(L)
    out = np.zeros((4, L), np.float32)
    for g, w in enumerate(POOL_WINDOWS):
        lo = np.clip(t - w // 2, 0, L)
        hi = np.clip(t + w - w // 2, 0, L)
        out[g] = (1.0 / (hi - lo).astype(np.float32)).astype(np.float32)
    return np.tile(out, (1, reps))


_NC_CACHE = {}


def kernel(x_prompt, x_sample, cache_mla_ckv, cache_mla_krope, cache_na_k, cache_na_v, c, c_ctx,
           ada_w, ada_b, ln_g, ln_b, pool_w_in, pool_w_grp, pool_scale, pool_w_out,
           mla_w_in, mla_q_norm, mla_w_uq, mla_kv_norm, mla_w_ukv, mla_w_out,
           na_w_in, na_rpb, na_w_out):
    f = lambda a: np.ascontiguousarray(np.asarray(a, dtype=np.float32))
    x_prompt, x_sample = f(x_prompt), f(x_sample)
    C, S, perm = _rope_tables()
    mla_w_in = f(mla_w_in)[0]
    mla_w_uq = f(mla_w_uq)[0]
    wq_pack = np.zeros((8, 512, 256), np.float32)
    for h in range(8):
        blk = mla_w_uq[:, h * 192:(h + 1) * 192]
        wq_pack[h, :, 0:192] = blk
        wq_pack[h, :, 192:256] = blk[:, 128:192][:, perm]
    w_krp = np.ascontiguousarray(mla_w_in[:, 768:832][:, perm])
    na_w = f(na_w_in)[0]
    na_pack = np.stack([np.concatenate([na_w[:, p * 1024 + hg * 128: p * 1024 + (hg + 1) * 128] for p in range(4)], 1)
                        for hg in range(8)], 0)
    rpb = f(na_rpb)[0]
    rpb_ext = np.concatenate([rpb.reshape(16, -1), np.full((16, 1), NEG, np.float32)], 1)
    idx = _na_table_index()
    na_tab = np.ascontiguousarray(rpb_ext[:, idx].transpose(1, 0, 2, 3, 4).reshape(2, 16, 128, NTAB * 64))
    rcs = _rcnt(2048, 1)
    shared = {
        "ada_w": f(ada_w), "ada_b": f(ada_b), "ln_g": f(ln_g), "ln_b": f(ln_b),
        "pool_w_in": f(pool_w_in), "pool_w_grp": f(pool_w_grp), "pool_scale": f(pool_scale), "pool_w_out": f(pool_w_out),
        "mla_w_in": mla_w_in, "mla_w_krp": w_krp, "mla_q_norm": f(mla_q_norm)[0], "mla_wq_pack": wq_pack,
        "mla_kv_norm": f(mla_kv_norm)[0], "mla_w_ukv": f(mla_w_ukv)[0], "mla_w_out": f(mla_w_out)[0],
        "na_w_pack": np.ascontiguousarray(na_pack), "na_w_out": f(na_w_out)[0], "na_tab": na_tab,
        "rope_cos": C, "rope_sin": S, "rcnt_p": _rcnt(256, 2), "rcnt_s": rcs,
        "ident": np.eye(128, dtype=np.float32),
    }
    c, c_ctx = f(c), f(c_ctx)
    in_maps = []
    wsel = [np.tile(np.array([[1.0, 0.0]], np.float32), (128, 1)), np.tile(np.array([[0.0, 1.0]], np.float32), (128, 1))]
    for core in range(8):
        b = core // 2
        m = dict(shared)
        m["xp"] = x_prompt[2 * core:2 * core + 2].reshape(512, D)
        m["xs"] = x_sample[b]
        m["cond"] = np.stack([c_ctx, c[b]], 0)
        m["c_ckv"] = f(cache_mla_ckv)[b, 0]
        m["c_kr"] = f(cache_mla_krope)[b, 0]
        m["c_nak"] = f(cache_na_k)[b, 0].reshape(256, D)
        m["c_nav"] = f(cache_na_v)[b, 0].reshape(256, D)
        w0 = 512 * (core % 2)
        m["wsel"] = wsel[core % 2]
        m["rcnt_w"] = np.ascontiguousarray(rcs[:, w0:w0 + 1536])
        m["rope_cos_w"] = np.ascontiguousarray(C[:, w0:w0 + 1536])
        m["rope_sin_w"] = np.ascontiguousarray(S[:, w0:w0 + 1536])
        in_maps.append(m)
    if "nc" not in _NC_CACHE:
        _NC_CACHE["nc"] = build_program()
    res = run_bass_kernel_spmd(_NC_CACHE["nc"], in_maps, core_ids=list(range(8)))
    R = res.results
    yp = np.concatenate([R[i]["yp"].reshape(2, 256, D) for i in range(8)], 0)
    ys = np.stack([np.concatenate([R[2 * b]["ys"][0:1024], R[2 * b + 1]["ys"][512:1536]], 0) for b in range(4)], 0)
    st_ckv = np.concatenate([R[i]["st_ckv"].reshape(2, 1, 256, 256) for i in range(8)], 0)
    st_kr = np.concatenate([R[i]["st_kr"].reshape(2, 1, 256, 64) for i in range(8)], 0)
    st_k = np.concatenate([R[i]["st_k"].reshape(2, 1, 256, 16, 64) for i in range(8)], 0)
    st_v = np.concatenate([R[i]["st_v"].reshape(2, 1, 256, 16, 64) for i in range(8)], 0)
    return (yp.astype(np.float32), ys.astype(np.float32), st_ckv.astype(np.float32), st_kr.astype(np.float32),
            st_k.astype(np.float32), st_v.astype(np.float32))
```
